# Optimizing a Trainium2 kernel written in Bass

```python
import jax, jax.numpy as jnp
from jax import lax
import numpy as np

D_MODEL = 2048
BATCH = 8
SEQ = 2048
DEPTH = 1
DEC_BATCH = 8
DEC_SEQ = 4096
PAST_LEN = 128

GRID_W = 64
ATT_HEADS = 8
ATT_KV_HEADS = 2
HEAD_DIM = 128
ATT_GROUP = ATT_HEADS // ATT_KV_HEADS
ATT_W = ATT_HEADS * HEAD_DIM
ATT_KV_W = ATT_KV_HEADS * HEAD_DIM
AXIS_DIM = HEAD_DIM // 2
ROPE_THETA = 10000.0
Q_BLOCK = 128
GLA_HEADS = 4
GLA_DK = 128
GLA_DV = 256
GLA_QK = GLA_HEADS * GLA_DK
GLA_V = GLA_HEADS * GLA_DV
GLA_RANK = 16
GATE_NORM = 16.0
GLA_CHUNK = 64
N_BRANCH = 2
D_FF = 5632
CONV_W = 3
EPS = 1e-6

IN_SIZES = [ATT_W, ATT_KV_W, ATT_KV_W, GLA_QK, GLA_QK, GLA_V, GLA_V, GLA_RANK, GLA_RANK, D_MODEL, D_MODEL]
IN_COLS = int(sum(IN_SIZES))
IN_SPLITS = [int(s) for s in np.cumsum(IN_SIZES)[:-1]]

kernel_name = "hybrid_gqa_gla_convffn_adaln_encoder"


def _rmsnorm(x, g):
    xf = x.astype(jnp.float32)
    y = xf * lax.rsqrt(jnp.mean(xf * xf, axis=-1, keepdims=True) + EPS)
    return (y * g.astype(jnp.float32)).astype(x.dtype)


def _rope_tables(T):
    rows = T // GRID_W
    row = jnp.repeat(jnp.arange(rows, dtype=jnp.float32), GRID_W)
    col = jnp.tile(jnp.arange(GRID_W, dtype=jnp.float32), rows)
    inv = ROPE_THETA ** (-jnp.arange(0, AXIS_DIM, 2, dtype=jnp.float32) / AXIS_DIM)
    ar = row[:, None] * inv
    ac = col[:, None] * inv
    ang = jnp.concatenate([ar, ar, ac, ac], axis=-1)
    return jnp.cos(ang), jnp.sin(ang)


def _apply_rope(x, cos, sin):
    xf = x.astype(jnp.float32)
    xr = xf.reshape(*x.shape[:-1], 2, 2, AXIS_DIM // 2)
    rot = jnp.stack([-xr[..., 1, :], xr[..., 0, :]], axis=-2).reshape(x.shape)
    return (xf * cos[:, None, :] + rot * sin[:, None, :]).astype(x.dtype)


def _attention(q, k, v):
    B, T = q.shape[0], q.shape[1]
    nb = T // Q_BLOCK
    qb = q.reshape(B, nb, Q_BLOCK, ATT_KV_HEADS, ATT_GROUP, HEAD_DIM).transpose(1, 0, 3, 4, 2, 5)
    kt = k.transpose(0, 2, 1, 3)
    vt = v.transpose(0, 2, 1, 3)
    scale = HEAD_DIM ** -0.5

    def blk(qi):
        s = jnp.einsum('bkgqd,bksd->bkgqs', qi, kt).astype(jnp.float32) * scale
        p = jax.nn.softmax(s, axis=-1).astype(vt.dtype)
        return jnp.einsum('bkgqs,bksd->bkgqd', p, vt)

    o = lax.map(blk, qb)
    return o.transpose(1, 0, 4, 2, 3, 5).reshape(B, T, ATT_W)


def _gla_direction(q, k, v, log_a):
    B, T, H, DK = q.shape
    DV = v.shape[-1]
    nc = T // GLA_CHUNK

    def chunks(a):
        return a.reshape(B, nc, GLA_CHUNK, H, a.shape[-1]).transpose(1, 0, 3, 2, 4)

    qc, kc, vc, gc = chunks(q), chunks(k), chunks(v), chunks(log_a)
    bc = jnp.cumsum(gc, axis=-2)
    mask = jnp.tril(jnp.ones((GLA_CHUNK, GLA_CHUNK), dtype=bool))

    def step(S, inp):
        qi, ki, vi, bi = inp
        q_e = qi * jnp.exp(bi)
        k_e = ki * jnp.exp(-bi)
        A = jnp.where(mask, jnp.einsum('bhtd,bhsd->bhts', q_e, k_e), 0.0)
        o = jnp.einsum('bhts,bhsv->bhtv', A, vi) + jnp.einsum('bhtd,bhdv->bhtv', q_e, S)
        b_last = bi[:, :, -1:, :]
        S = jnp.exp(b_last[:, :, 0, :, None]) * S + jnp.einsum('bhsd,bhsv->bhdv', ki * jnp.exp(b_last - bi), vi)
        return S, o

    S0 = jnp.zeros((B, H, DK, DV), jnp.float32)
    _, o = lax.scan(step, S0, (qc, kc, vc, bc))
    return o.transpose(1, 0, 3, 2, 4).reshape(B, T, H, DV)


def _gla_mixer(q, k, v, r, low_f, low_b, w_a_up_f, b_a_f, w_a_up_b, b_a_b, g_gla):
    B, T = q.shape[0], q.shape[1]
    f32 = jnp.float32
    qh = q.astype(f32).reshape(B, T, GLA_HEADS, GLA_DK) * (GLA_DK ** -0.5)
    kh = k.astype(f32).reshape(B, T, GLA_HEADS, GLA_DK)
    vh = v.astype(f32).reshape(B, T, GLA_HEADS, GLA_DV)
    la_f = (jax.nn.log_sigmoid((low_f @ w_a_up_f + b_a_f).astype(f32)) / GATE_NORM).reshape(B, T, GLA_HEADS, GLA_DK)
    la_b = (jax.nn.log_sigmoid((low_b @ w_a_up_b + b_a_b).astype(f32)) / GATE_NORM).reshape(B, T, GLA_HEADS, GLA_DK)
    o_f = _gla_direction(qh, kh, vh, la_f)
    o_b = _gla_direction(qh[:, ::-1], kh[:, ::-1], vh[:, ::-1], la_b[:, ::-1])[:, ::-1]
    diag = jnp.sum(qh * kh, axis=-1, keepdims=True) * vh
    o = _rmsnorm(o_f + o_b - diag, g_gla.reshape(GLA_HEADS, GLA_DV))
    return o.reshape(B, T, GLA_V).astype(r.dtype) * jax.nn.silu(r)


def _layer(x, c, w_mod, b_mod, g_mix_norm, w_in, g_q, g_k, w_a_up_f, b_a_f, w_a_up_b, b_a_b,
           g_gla, w_br_att, w_br_gla, w_out, g_ffn_norm, w_up, w_conv, b_conv, w_down):
    B, T, _ = x.shape
    mod = (jax.nn.silu(c) @ w_mod + b_mod)[:, None, :]
    sh1, sc1, gt1, sh2, sc2, gt2 = jnp.split(mod, 6, axis=-1)

    h = _rmsnorm(x, g_mix_norm) * (1 + sc1) + sh1
    proj = h @ w_in
    (aq, ak, av, gq, gk, gv, gr, low_f, low_b, gate_a, gate_g) = jnp.split(proj, IN_SPLITS, axis=-1)

    cos, sin = _rope_tables(T)
    qh = _apply_rope(_rmsnorm(aq.reshape(B, T, ATT_HEADS, HEAD_DIM), g_q), cos, sin)
    kh = _apply_rope(_rmsnorm(ak.reshape(B, T, ATT_KV_HEADS, HEAD_DIM), g_k), cos, sin)
    vh = av.reshape(B, T, ATT_KV_HEADS, HEAD_DIM)
    att = _attention(qh, kh, vh)

    gla = _gla_mixer(gq, gk, gv, gr, low_f, low_b, w_a_up_f, b_a_f, w_a_up_b, b_a_b, g_gla)

    merged = jax.nn.sigmoid(gate_a) * (att @ w_br_att) + jax.nn.sigmoid(gate_g) * (gla @ w_br_gla)
    x = x + gt1 * (merged @ w_out)

    h2 = _rmsnorm(x, g_ffn_norm) * (1 + sc2) + sh2
    u = h2 @ w_up
    up = jnp.pad(u, ((0, 0), (1, 1), (0, 0)))
    u = w_conv[0] * up[:, :-2] + w_conv[1] * up[:, 1:-1] + w_conv[2] * up[:, 2:] + b_conv
    val, gate = jnp.split(u, 2, axis=-1)
    x = x + gt2 * ((jax.nn.silu(gate) * val) @ w_down)
    return x


def _trunk(x, c, w_mod, b_mod, g_mix_norm, w_in, g_q, g_k, w_a_up_f, b_a_f, w_a_up_b, b_a_b,
           g_gla, w_br_att, w_br_gla, w_out, g_ffn_norm, w_up, w_conv, b_conv, w_down, g_final):
    for l in range(DEPTH):
        x = _layer(x, c, w_mod[l], b_mod[l], g_mix_norm[l], w_in[l], g_q[l], g_k[l],
                   w_a_up_f[l], b_a_f[l], w_a_up_b[l], b_a_b[l], g_gla[l], w_br_att[l],
                   w_br_gla[l], w_out[l], g_ffn_norm[l], w_up[l], w_conv[l], b_conv[l], w_down[l])
    return _rmsnorm(x, g_final)


def setup_inputs(seed: int = 0) -> dict:
    key = jax.random.key(seed)
    ks = jax.random.split(key, 32)
    f32 = jnp.float32
    L, D = DEPTH, D_MODEL

    def nrm(k, shape, scale):
        return jax.random.normal(k, shape, f32) * scale

    def gain(k, shape):
        return 1.0 + 0.02 * jax.random.normal(k, shape, f32)

    return {
        "x_prompt": nrm(ks[0], (BATCH, SEQ, D), 1.0),
        "x_sample": nrm(ks[1], (DEC_BATCH, DEC_SEQ, D), 1.0),
        "c_prompt": nrm(ks[2], (BATCH, D), 1.0),
        "c_sample": nrm(ks[3], (DEC_BATCH, D), 1.0),
        "w_mod": nrm(ks[4], (L, D, 6 * D), 0.5 * D ** -0.5),
        "b_mod": nrm(ks[5], (L, 6 * D), 0.01),
        "g_mix_norm": gain(ks[6], (L, D)),
        "w_in": nrm(ks[7], (L, D, IN_COLS), D ** -0.5),
        "g_q": gain(ks[8], (L, HEAD_DIM)),
        "g_k": gain(ks[9], (L, HEAD_DIM)),
        "w_a_up_f": nrm(ks[10], (L, GLA_RANK, GLA_QK), GLA_RANK ** -0.5),
        "b_a_f": nrm(ks[11], (L, GLA_QK), 0.1),
        "w_a_up_b": nrm(ks[12], (L, GLA_RANK, GLA_QK), GLA_RANK ** -0.5),
        "b_a_b": nrm(ks[13], (L, GLA_QK), 0.1),
        "g_gla": gain(ks[14], (L, GLA_V)),
        "w_br_att": nrm(ks[15], (L, ATT_W, D), ATT_W ** -0.5),
        "w_br_gla": nrm(ks[16], (L, GLA_V, D), GLA_V ** -0.5),
        "w_out": nrm(ks[17], (L, D, D), D ** -0.5),
        "g_ffn_norm": gain(ks[18], (L, D)),
        "w_up": nrm(ks[19], (L, D, 2 * D_FF), D ** -0.5),
        "w_conv": nrm(ks[20], (L, CONV_W, 2 * D_FF), CONV_W ** -0.5),
        "b_conv": nrm(ks[21], (L, 2 * D_FF), 0.01),
        "w_down": nrm(ks[22], (L, D_FF, D), D_FF ** -0.5),
        "g_final": gain(ks[23], (D,)),
    }


def reference(x_prompt, x_sample, c_prompt, c_sample, w_mod, b_mod, g_mix_norm, w_in, g_q, g_k,
              w_a_up_f, b_a_f, w_a_up_b, b_a_b, g_gla, w_br_att, w_br_gla, w_out, g_ffn_norm,
              w_up, w_conv, b_conv, w_down, g_final):
    y_prompt = _trunk(x_prompt, c_prompt, w_mod, b_mod, g_mix_norm, w_in, g_q, g_k, w_a_up_f, b_a_f,
                      w_a_up_b, b_a_b, g_gla, w_br_att, w_br_gla, w_out, g_ffn_norm, w_up, w_conv,
                      b_conv, w_down, g_final)
    y_sample = _trunk(x_sample, c_sample, w_mod, b_mod, g_mix_norm, w_in, g_q, g_k, w_a_up_f, b_a_f,
                      w_a_up_b, b_a_b, g_gla, w_br_att, w_br_gla, w_out, g_ffn_norm, w_up, w_conv,
                      b_conv, w_down, g_final)
    return (y_prompt, y_sample)
```

```python
import numpy as np
from contextlib import ExitStack
import concourse.bass as bass
import concourse.mybir as mybir
from concourse.bass_utils import run_bass_kernel_spmd
from concourse.alu_op_type import AluOpType as ALU

F32 = mybir.dt.float32
BF16 = mybir.dt.bfloat16
AF = mybir.ActivationFunctionType
AX = mybir.AxisListType

D = 2048
KC = 16
IN_COLS = 8736
DFF = 5632
NFC = 44
EPS = 1e-6
C_AQ, C_AK, C_AV, C_GQ, C_GK, C_GV, C_GR, C_LOW, C_GA, C_GG = 0, 1024, 1280, 1536, 2048, 2560, 3584, 4608, 4640, 6688
N_CORES = 8


class Buf:
    __slots__ = ("w", "r")

    def __init__(self):
        self.w = None
        self.r = {}


class DS:
    def __init__(self, sem, name):
        self.sem = sem
        self.n = 0
        self.name = name


class Ctx:
    def __init__(self, nc, es):
        self.nc = nc
        self.es = es
        self.E = dict(pe=nc.tensor, act=nc.scalar, dve=nc.vector, pool=nc.gpsimd, sp=nc.sync)
        self.csem = {e: es.enter_context(nc.semaphore("cs_" + e)) for e in ("pe", "act", "dve", "pool")}
        self.cnt = dict(pe=0, act=0, dve=0, pool=0)
        self.waited = {e: {} for e in self.E}
        self.ds_pool = [DS(es.enter_context(nc.semaphore("ds%d" % i)), "ds%d" % i) for i in range(72)]
        self.ds_next = 0
        self.bank = []
        self.bankb = []
        for i in range(8):
            self.bank.append(es.enter_context(nc.psum_tensor("bank%d" % i, [128, 512], F32)))
            self.bankb.append(Buf())

    def ds(self):
        d = self.ds_pool[self.ds_next % len(self.ds_pool)]
        self.ds_next += 1
        return d

    def nm(self, name):
        self.uid = getattr(self, "uid", 0) + 1
        return "%s_u%d" % (name, self.uid)

    def sb(self, es, name, shape, dtype):
        t = es.enter_context(self.nc.sbuf_tensor(self.nm(name), list(shape), dtype))
        return t, Buf()

    def _collect(self, reads, writes):
        deps = {}

        def add(tok):
            k = tok[0]
            if k not in deps or deps[k][1] < tok[2]:
                deps[k] = (tok[1], tok[2], tok[3])

        for b in reads:
            if b.w is not None:
                add(b.w)
        for b in writes:
            if b.w is not None:
                add(b.w)
            for tok in b.r.values():
                add(tok)
        return deps

    def _wait(self, e, deps):
        w = self.waited[e]
        eng = self.E[e]
        for k, (sem, val, src) in deps.items():
            if src == "pe" and e == "pe":
                continue
            if w.get(k, 0) >= val:
                continue
            eng.wait_ge(sem, val)
            w[k] = val

    def _record(self, tok, reads, writes):
        for b in reads:
            b.r[tok[0]] = tok
        for b in writes:
            b.w = tok
            b.r = {}

    def op(self, e, fn, reads=(), writes=()):
        self._wait(e, self._collect(reads, writes))
        inst = fn(self.E[e])
        self.cnt[e] += 1
        inst.then_inc(self.csem[e], 1)
        self._record(("c_" + e, self.csem[e], self.cnt[e], e), reads, writes)

    def dma(self, q, out, in_, ds, reads=(), writes=()):
        self._wait(q, self._collect(reads, writes))
        ds.n += 1
        self.E[q].dma_start(out=out, in_=in_).then_inc(ds.sem, 16)
        self._record(("d_" + ds.name, ds.sem, 16 * ds.n, "dma"), reads, writes)

    def barrier(self):
        deps = {}
        for e in self.cnt:
            if self.cnt[e]:
                deps["c_" + e] = (self.csem[e], self.cnt[e], e)
        for d in self.ds_pool:
            if d.n:
                deps["d_" + d.name] = (d.sem, 16 * d.n, "dma")
        for e in self.E:
            self._wait(e, deps)

    def mm(self, out, ob, lhsT, rhs, start, stop, reads):
        self.op("pe", lambda e: e.matmul(out, lhsT=lhsT, rhs=rhs, start=start, stop=stop), reads=reads, writes=[ob])

    def tr(self, out, ob, in_, ident, reads):
        self.op("pe", lambda e: e.transpose(out=out, in_=in_, identity=ident), reads=reads, writes=[ob])


def kc_view(w, c0, c1):
    return w.rearrange("(kc p) n -> p kc n", p=128)[:, :, c0:c1]


class G:
    pass


def prep_block(K, g, xblk, xb, A, B, ab, hT, hTb, wk):
    junk, junkb, ssq, ssqb = wk
    for i in range(4):
        K.op("act", lambda e: e.activation(out=junk[:], in_=xblk[:, i, :], func=AF.Square, accum_out=ssq[:, i:i + 1]),
             reads=[xb[i]], writes=[junkb, ssqb])
    K.op("dve", lambda e: e.tensor_scalar(out=ssq[:, 0:4], in0=ssq[:, 0:4], scalar1=1.0 / D, scalar2=EPS, op0=ALU.mult, op1=ALU.add),
         reads=[ssqb], writes=[ssqb])
    K.op("act", lambda e: e.activation(out=ssq[:, 0:4], in_=ssq[:, 0:4], func=AF.Sqrt), reads=[ssqb], writes=[ssqb])
    K.op("dve", lambda e: e.reciprocal(out=ssq[:, 4:8], in_=ssq[:, 0:4]), reads=[ssqb], writes=[ssqb])
    for i in range(4):
        K.op("dve", lambda e: e.tensor_scalar(out=xblk[:, i, :], in0=xblk[:, i, :], scalar1=ssq[:, 4 + i:5 + i], scalar2=None, op0=ALU.mult),
             reads=[xb[i], ssqb], writes=[xb[i]])
    for c in range(KC):
        bk = c % 2
        for i in range(4):
            K.tr(K.bank[bk][:, i * 128:(i + 1) * 128], K.bankb[bk], xblk[:, i, c * 128:(c + 1) * 128], g.ident_f[:], [xb[i], g.cb])
        K.op("act", lambda e: e.activation(out=hT[:, c, :], in_=K.bank[bk][:, :], func=AF.Identity, scale=A[:, c:c + 1], bias=B[:, c:c + 1]),
             reads=[K.bankb[bk], ab], writes=[hTb])


def phase_prep(K, g, s, T, src, dst, aidx):
    nc = K.nc
    NB = T // 512
    with ExitStack() as es:
        xblks = [es.enter_context(nc.sbuf_tensor(K.nm("p_xblk"), [128, 4, D], F32)) for _ in range(2)]
        xbs = [[Buf() for _ in range(4)] for _ in range(2)]
        dx = [[K.ds() for _ in range(4)] for _ in range(2)]
        hTs = [K.sb(es, "p_hT", [128, KC, 512], BF16) for _ in range(2)]
        dh = [K.ds() for _ in range(2)]
        wk = [K.sb(es, "p_junk", [128, D], BF16) + K.sb(es, "p_ssq", [128, 8], F32) for _ in range(2)]
        for blk in range(NB):
            t0 = blk * 512
            j = blk % 2
            for i in range(4):
                K.dma("sp", xblks[j][:, i, :], src[t0 + i * 128:t0 + (i + 1) * 128, :], dx[j][i], writes=[xbs[j][i]])
            hT, hTb = hTs[j]
            prep_block(K, g, xblks[j], xbs[j], g.AB[:, s, aidx, :], g.AB[:, s, aidx + 1, :], g.cb, hT, hTb, wk[j])
            K.dma("sp", dst.rearrange("c p t -> p c t")[:, :, t0:t0 + 512], hT[:], dh[j], reads=[hTb])
        K.barrier()


def phase_convert(K, g):
    with ExitStack() as es:
        stg = [K.sb(es, "cv%d" % i, [128, 11264], BF16) for i in range(3)]
        dl = [K.ds() for _ in range(3)]
        dst_ = [K.ds() for _ in range(3)]
        i = 0
        for src, dst, Kd, N in g.conv_pairs:
            for kc in range(Kd // 128):
                t, b = stg[i % 3]
                K.dma("pool", t[:, 0:N], src[kc * 128:(kc + 1) * 128, :], dl[i % 3], writes=[b])
                K.dma("sp", dst[kc * 128:(kc + 1) * 128, :], t[:, 0:N], dst_[i % 3], reads=[b])
                i += 1
        K.barrier()


def phase_setup(K, g, es):
    nc = K.nc
    g.cb = Buf()
    d0 = K.ds()

    def ld(name, shape, dtype, src, q="sp"):
        t = es.enter_context(nc.sbuf_tensor(name, list(shape), dtype))
        K.dma(q, t[:], src, d0, writes=[g.cb])
        return t

    g.ident_f = ld("ident_f", [128, 128], F32, g.d_ident[:, :])
    g.masks = ld("masks", [128, 2, 128], F32, g.d_masks.rearrange("m p t -> p m t"))
    g.tri = ld("tri", [128, 4, 128], F32, g.d_tri.rearrange("m p t -> p m t"))
    g.gq_b = ld("gq_b", [128, 128], F32, g.w["g_q"][0].partition_broadcast(128))
    g.gk_b = ld("gk_b", [128, 128], F32, g.w["g_k"][0].partition_broadcast(128))
    g.wa = []
    for nm, wn, bn in (("f", "w_a_up_f", "b_a_f"), ("b", "w_a_up_b", "b_a_b")):
        t = es.enter_context(nc.sbuf_tensor("wa_" + nm, [17, 512], F32))
        K.dma("sp", t[0:16, :], g.w[wn][0], d0, writes=[g.cb])
        K.dma("sp", t[16:17, :], g.w[bn][0:1, :], d0, writes=[g.cb])
        g.wa.append(t)
    g.ident_b = es.enter_context(nc.sbuf_tensor("ident_b", [128, 128], BF16))
    g.ones_b = es.enter_context(nc.sbuf_tensor("ones_b", [128, 128], BF16))
    K.op("dve", lambda e: e.tensor_copy(out=g.ident_b[:], in_=g.ident_f[:]), reads=[g.cb], writes=[g.cb])
    K.op("dve", lambda e: e.memset(g.ones_b[:], 1.0), writes=[g.cb])
    g.nshift = es.enter_context(nc.sbuf_tensor("nshift", [128, 2], F32))
    K.op("dve", lambda e: e.tensor_reduce(out=g.nshift[:, 0:1], in_=g.gq_b[:], axis=AX.X, op=ALU.max, apply_absolute_value=True), reads=[g.cb], writes=[g.cb])
    K.op("dve", lambda e: e.tensor_reduce(out=g.nshift[:, 1:2], in_=g.gk_b[:], axis=AX.X, op=ALU.max, apply_absolute_value=True), reads=[g.cb], writes=[g.cb])
    K.op("dve", lambda e: e.tensor_tensor(out=g.nshift[:, 0:1], in0=g.nshift[:, 0:1], in1=g.nshift[:, 1:2], op=ALU.mult), reads=[g.cb], writes=[g.cb])
    K.op("dve", lambda e: e.tensor_scalar(out=g.nshift[:, 0:1], in0=g.nshift[:, 0:1], scalar1=-(128.0 ** 0.5), scalar2=None, op0=ALU.mult), reads=[g.cb], writes=[g.cb])

    g.vecT = es.enter_context(nc.sbuf_tensor("vecT", [128, 96 + 16 + 16 + 4 * 88], F32))
    g.modT = es.enter_context(nc.sbuf_tensor("modT", [128, 2, 96], F32))
    g.AB = es.enter_context(nc.sbuf_tensor("AB", [128, 2, 4, 16], F32))
    with ExitStack() as es2:
        rows = es2.enter_context(nc.sbuf_tensor("rows", [96, 7, 128], F32))
        rb = Buf()
        K.dma("sp", rows[0:96, 0, :], g.w["b_mod"][0].rearrange("(c p) -> c p", p=128), d0, writes=[rb])
        K.dma("sp", rows[0:16, 1, :], g.w["g_mix_norm"][0].rearrange("(c p) -> c p", p=128), d0, writes=[rb])
        K.dma("sp", rows[0:16, 2, :], g.w["g_ffn_norm"][0].rearrange("(c p) -> c p", p=128), d0, writes=[rb])
        for r in range(3):
            K.dma("sp", rows[0:88, 3 + r, :], g.w["w_conv"][0, r].rearrange("(c p) -> c p", p=128), d0, writes=[rb])
        K.dma("sp", rows[0:88, 6, :], g.w["b_conv"][0].rearrange("(c p) -> c p", p=128), d0, writes=[rb])
        specs = [(0, 96, 0), (1, 16, 96), (2, 16, 112), (3, 88, 128), (4, 88, 216), (5, 88, 304), (6, 88, 392)]
        for r, n, off in specs:
            K.tr(K.bank[0][:, off % 512:off % 512 + n], K.bankb[0], rows[0:n, r, :], g.ident_f[0:n, 0:n], [rb, g.cb])
        K.op("dve", lambda e: e.tensor_copy(out=g.vecT[:, 0:480], in_=K.bank[0][:, 0:480]), reads=[K.bankb[0]], writes=[g.cb])
        crow = es2.enter_context(nc.sbuf_tensor("crow", [16, 2, 128], F32))
        K.dma("sp", crow[:], g.d_c2.rearrange("s (kc p) -> kc s p", p=128), d0, writes=[rb])
        g_scT = es2.enter_context(nc.sbuf_tensor("scT", [128, 2, 16], BF16))
        for s in range(2):
            K.tr(K.bank[1][:, s * 16:(s + 1) * 16], K.bankb[1], crow[:, s, :], g.ident_f[0:16, 0:16], [rb, g.cb])
        K.op("act", lambda e: e.activation(out=g_scT[:].rearrange("p s k -> p (s k)"), in_=K.bank[1][:, 0:32], func=AF.Silu), reads=[K.bankb[1]], writes=[rb])
        wm = [K.sb(es2, "wm%d" % i, [128, 16, 1024], BF16) for i in range(2)]
        dm = [K.ds() for _ in range(2)]
        wmod = g.w["w_mod"][0]
        for cb in range(12):
            t, b = wm[cb % 2]
            for q4 in range(4):
                K.dma("pool", t[:, q4 * 4:(q4 + 1) * 4, :], kc_view(wmod, cb * 1024, (cb + 1) * 1024)[:, q4 * 4:(q4 + 1) * 4, :], dm[cb % 2], writes=[b])
            for jl in range(8):
                j = cb * 8 + jl
                for kc in range(KC):
                    K.mm(K.bank[2][:, 2 * j:2 * j + 2], K.bankb[2], t[:, kc, jl * 128:(jl + 1) * 128], g_scT[:, :, kc], kc == 0, kc == KC - 1, [b, rb])
        for s in range(2):
            K.op("dve", lambda e: e.tensor_tensor(out=g.modT[:, s, :], in0=K.bank[2][:, 0:192].rearrange("p (j s) -> p s j", s=2)[:, s, :], in1=g.vecT[:, 0:96], op=ALU.add),
                 reads=[K.bankb[2], g.cb], writes=[g.cb])
        for s in range(2):
            K.op("dve", lambda e: e.scalar_tensor_tensor(out=g.AB[:, s, 0, :], in0=g.modT[:, s, 16:32], scalar=1.0, in1=g.vecT[:, 96:112], op0=ALU.add, op1=ALU.mult), reads=[g.cb], writes=[g.cb])
            K.op("dve", lambda e: e.tensor_copy(out=g.AB[:, s, 1, :], in_=g.modT[:, s, 0:16]), reads=[g.cb], writes=[g.cb])
            K.op("dve", lambda e: e.scalar_tensor_tensor(out=g.AB[:, s, 2, :], in0=g.modT[:, s, 64:80], scalar=1.0, in1=g.vecT[:, 112:128], op0=ALU.add, op1=ALU.mult), reads=[g.cb], writes=[g.cb])
            K.op("dve", lambda e: e.tensor_copy(out=g.AB[:, s, 3, :], in_=g.modT[:, s, 48:64]), reads=[g.cb], writes=[g.cb])
        K.barrier()


def make_gate_bcast(K, g, es, s, which, name):
    t, b = K.sb(es, name, [128, D], F32)
    base = 32 if which == 0 else 80
    with ExitStack() as es2:
        rep, rb = K.sb(es2, name + "_rep", [128, 2, 128], F32)
        for c in range(KC):
            bk = 3 + (c // 4) % 2
            K.op("dve", lambda e: e.tensor_copy(out=rep[:, c % 2, :], in_=g.modT[:, s, base + c:base + c + 1].broadcast_to([128, 128])), reads=[g.cb], writes=[rb])
            K.mm(K.bank[bk][:, (c % 4) * 128:(c % 4 + 1) * 128], K.bankb[bk], rep[:, c % 2, :], g.ident_f[:], True, True, [rb, g.cb])
            if c % 4 == 3:
                K.op("act", lambda e: e.copy(out=t[:, (c - 3) * 128:(c + 1) * 128], in_=K.bank[bk][:, :]), reads=[K.bankb[bk]], writes=[b])
        K.barrier()
    return t, b


def phase1(K, g, s, T, x_in):
    nc = K.nc
    NB = T // 512
    win = g.wbf["w_in"]
    groups = [(0, 512, "q0"), (512, 512, "q1"), (1024, 512, "kv"), (C_GQ, 512, "gqT"), (C_GK, 512, "gk"),
              (C_GV, 512, "gv0"), (C_GV + 512, 512, "gv1"), (C_GR, 512, "gr0"), (C_GR + 512, 512, "gr1"), (C_LOW, 32, "low")]
    with ExitStack() as es:
        hTs = [K.sb(es, "hT%d" % i, [128, KC, 512], BF16) for i in range(2)]
        dh = [K.ds() for _ in range(2)]
        wsl = [K.sb(es, "wsl%d" % i, [128, KC, 512], BF16) for i in range(3)]
        dw = [K.ds() for _ in range(3)]
        css = [K.sb(es, "cs", [128, 2, 4, 128], F32) for _ in range(2)]
        dcs = [K.ds() for _ in range(2)]
        pending = []
        sq, sqb = K.sb(es, "sq", [128, 512], F32)
        ssh, sshb = K.sb(es, "ssh", [128, 8], F32)
        qn = [K.sb(es, "qn%d" % i, [128, 512], F32) for i in range(2)]
        t1 = [K.sb(es, "t1%d" % i, [128, 512], F32) for i in range(2)]
        t2 = [K.sb(es, "t2%d" % i, [128, 512], F32) for i in range(2)]
        qr = [K.sb(es, "qr%d" % i, [128, 512], BF16) for i in range(3)]
        qTst, qTstb = K.sb(es, "qTst", [128, 8, 512], BF16)
        kTst, kTstb = K.sb(es, "kTst", [128, 2, 512], BF16)
        dqk = [K.ds() for _ in range(2)]
        stb = [K.sb(es, "stb%d" % i, [128, 512], BF16) for i in range(6)]
        dsb = [K.ds() for _ in range(6)]
        stf = [K.sb(es, "stf%d" % i, [128, 512], F32) for i in range(4)]
        dsf = [K.ds() for _ in range(4)]
        cnt = dict(w=0, mb=0, qk=0, stb=0, stf=0)

        def flush(keep):
            while len(pending) > keep:
                pending.pop(0)()

        def load_act(blk):
            t0_ = blk * 512
            hT_, hTb_ = hTs[blk % 2]
            K.dma("sp", hT_[:], g.sc["hT"].rearrange("c p t -> p c t")[:, :, t0_:t0_ + 512], dh[blk % 2], writes=[hTb_])
            cs_, csb_ = css[blk % 2]
            K.dma("sp", cs_[:, 0, :, :], g.d_cos[t0_:t0_ + 512, :].rearrange("(i p) d -> p i d", p=128), dcs[blk % 2], writes=[csb_])
            K.dma("sp", cs_[:, 1, :, :], g.d_sin[t0_:t0_ + 512, :].rearrange("(i p) d -> p i d", p=128), dcs[blk % 2], writes=[csb_])

        def load_w(gi_global):
            c0, ncol, _ = groups[gi_global % len(groups)]
            j = cnt["w"] % 3
            cnt["w"] += 1
            t, b = wsl[j]
            K.dma("sp", t[:, :, 0:ncol], kc_view(win, c0, c0 + ncol), dw[j], writes=[b])
            return t, b

        def qk_post(bk, nh, hbase, gb, i, stage, stageb, cs, csb):
            W = nh * 128
            P = K.bank[bk][:, 0:W]
            P3 = P.rearrange("p (h d) -> p h d", h=nh)
            j = cnt["qk"] % 2
            cnt["qk"] += 1
            qn_t, qn_b = qn[j]
            t1_t, t1_b = t1[j]
            t2_t, t2_b = t2[j]
            qr_t, qr_b = qr[(cnt["qk"] - 1) % 3]
            K.op("act", lambda e: e.activation(out=sq[:, 0:W], in_=P, func=AF.Square), reads=[K.bankb[bk]], writes=[sqb])
            K.op("dve", lambda e: e.tensor_reduce(out=ssh[:, 0:nh], in_=sq[:, 0:W].rearrange("p (h d) -> p h d", h=nh), axis=AX.X, op=ALU.add), reads=[sqb], writes=[sshb])
            K.op("dve", lambda e: e.tensor_scalar(out=ssh[:, 0:nh], in0=ssh[:, 0:nh], scalar1=1.0 / 128, scalar2=EPS, op0=ALU.mult, op1=ALU.add), reads=[sshb], writes=[sshb])
            K.op("act", lambda e: e.activation(out=ssh[:, 0:nh], in_=ssh[:, 0:nh], func=AF.Sqrt), reads=[sshb], writes=[sshb])
            K.op("dve", lambda e: e.reciprocal(out=ssh[:, 4:4 + nh], in_=ssh[:, 0:nh]), reads=[sshb], writes=[sshb])
            qn3 = qn_t[:, 0:W].rearrange("p (h d) -> p h d", h=nh)
            K.op("dve", lambda e: e.tensor_tensor(out=qn3, in0=P3, in1=ssh[:, 4:4 + nh].unsqueeze(2).broadcast_to([128, nh, 128]), op=ALU.mult),
                 reads=[K.bankb[bk], sshb], writes=[qn_b])
            K.op("pool", lambda e: e.tensor_tensor(out=qn3, in0=qn3, in1=gb[:].unsqueeze(1).broadcast_to([128, nh, 128]), op=ALU.mult), reads=[qn_b, g.cb], writes=[qn_b])
            K.op("pool", lambda e: e.tensor_tensor(out=t1_t[:, 0:W].rearrange("p (h d) -> p h d", h=nh), in0=qn3, in1=cs[:, 0, i, :].unsqueeze(1).broadcast_to([128, nh, 128]), op=ALU.mult),
                 reads=[qn_b, csb], writes=[t1_b])
            qn5 = qn_t[:, 0:W].rearrange("p (h a f i) -> p h a f i", h=nh, a=2, f=2)
            t25 = t2_t[:, 0:W].rearrange("p (h a f i) -> p h a f i", h=nh, a=2, f=2)
            sn4 = cs[:, 1, i, :].rearrange("p (a f i) -> p a f i", a=2, f=2)
            for f in range(2):
                K.op("dve", lambda e: e.tensor_tensor(out=t25[:, :, :, f, :], in0=qn5[:, :, :, 1 - f, :], in1=sn4[:, :, f, :].unsqueeze(1).broadcast_to([128, nh, 2, 32]), op=ALU.mult),
                     reads=[qn_b, csb], writes=[t2_b])
            K.op("dve", lambda e: e.tensor_tensor(out=qr_t[:, 0:W], in0=t1_t[:, 0:W], in1=t2_t[:, 0:W], op=ALU.add), reads=[t1_b, t2_b], writes=[qr_b])

            def tail():
                tb = K.bank[6][:].bitcast(BF16)
                for h in range(nh):
                    K.tr(tb[:, h * 128:(h + 1) * 128], K.bankb[6], qr_t[:, h * 128:(h + 1) * 128], g.ident_b[:], [qr_b, g.cb])
                K.op("act", lambda e: e.copy(out=stage[:, hbase:hbase + nh, i * 128:(i + 1) * 128], in_=tb[:, 0:W].rearrange("p (h t) -> p h t", h=nh)),
                     reads=[K.bankb[6]], writes=[stageb])

            pending.append(tail)

        def store_b(fn_evac, dst, ncols=512):
            j = cnt["stb"] % 6
            cnt["stb"] += 1
            t, b = stb[j]
            fn_evac(t, b)
            K.dma("sp", dst, t[:, 0:ncols], dsb[j], reads=[b])

        def store_f(fn_evac, dst, npart=128):
            j = cnt["stf"] % 4
            cnt["stf"] += 1
            t, b = stf[j]
            fn_evac(t, b)
            K.dma("sp", dst, t[0:npart, :], dsf[j], reads=[b])

        load_act(0)
        wq = [load_w(0), load_w(1)]
        for blk in range(NB):
            t0 = blk * 512
            hT, hTb = hTs[blk % 2]
            cs, csb = css[blk % 2]
            if blk + 1 < NB:
                load_act(blk + 1)
            for gi, (c0, ncol, kind) in enumerate(groups):
                wt, wb = wq.pop(0)
                if gi + 2 < len(groups) or blk + 1 < NB:
                    wq.append(load_w(gi + 2))
                if kind in ("gqT", "gk"):
                    dstT = g.sc["gqT"] if kind == "gqT" else g.sc["gkT"]
                    for m in range(4):
                        bk = 2 + cnt["mb"] % 4
                        cnt["mb"] += 1
                        for kc in range(KC):
                            K.mm(K.bank[bk][:, :], K.bankb[bk], wt[:, kc, m * 128:(m + 1) * 128], hT[:, kc, :], kc == 0, kc == KC - 1, [wb, hTb])
                        sc_ = (128.0 ** -0.5) if kind == "gqT" else 1.0
                        store_b(lambda t, b: K.op("act", lambda e: e.activation(out=t[:], in_=K.bank[bk][:, :], func=AF.Copy, scale=sc_), reads=[K.bankb[bk]], writes=[b]),
                                dstT[m, :, t0:t0 + 512])
                if kind == "low":
                    for dr in range(2):
                        bk = 2 + cnt["mb"] % 4
                        cnt["mb"] += 1
                        for kc in range(KC):
                            K.mm(K.bank[bk][0:16, :], K.bankb[bk], wt[:, kc, dr * 16:(dr + 1) * 16], hT[:, kc, :], kc == 0, kc == KC - 1, [wb, hTb])
                        store_f(lambda t, b: K.op("act", lambda e: e.copy(out=t[0:16, :], in_=K.bank[bk][0:16, :]), reads=[K.bankb[bk]], writes=[b]),
                                g.sc["lowT"][dr, :, t0:t0 + 512], npart=16)
                if kind in ("q0", "q1", "kv", "gk", "gv0", "gv1", "gr0", "gr1"):
                    for i in range(4):
                        bk = 2 + cnt["mb"] % 4
                        cnt["mb"] += 1
                        for kc in range(KC):
                            K.mm(K.bank[bk][:, :], K.bankb[bk], hT[:, kc, i * 128:(i + 1) * 128], wt[:, kc, :], kc == 0, kc == KC - 1, [wb, hTb])
                        flush(1)
                        r0 = t0 + i * 128
                        if kind == "q0":
                            qk_post(bk, 4, 0, g.gq_b, i, qTst, qTstb, cs, csb)
                        elif kind == "q1":
                            qk_post(bk, 4, 4, g.gq_b, i, qTst, qTstb, cs, csb)
                        elif kind == "kv":
                            qk_post(bk, 2, 0, g.gk_b, i, kTst, kTstb, cs, csb)
                            store_b(lambda t, b: K.op("dve", lambda e: e.tensor_copy(out=t[:, 0:256], in_=K.bank[bk][:, 256:512]), reads=[K.bankb[bk]], writes=[b]),
                                    g.sc["v"][r0:r0 + 128, :], ncols=256)
                        elif kind == "gk":
                            store_b(lambda t, b: K.op("dve", lambda e: e.tensor_copy(out=t[:], in_=K.bank[bk][:, :]), reads=[K.bankb[bk]], writes=[b]),
                                    g.sc["gk"][r0:r0 + 128, :])
                        elif kind in ("gv0", "gv1"):
                            o = 0 if kind == "gv0" else 512
                            store_b(lambda t, b: K.op("dve", lambda e: e.tensor_copy(out=t[:], in_=K.bank[bk][:, :]), reads=[K.bankb[bk]], writes=[b]),
                                    g.sc["gv"][r0:r0 + 128, o:o + 512])
                        else:
                            o = 0 if kind == "gr0" else 512
                            store_f(lambda t, b: K.op("act", lambda e: e.activation(out=t[:], in_=K.bank[bk][:, :], func=AF.Silu), reads=[K.bankb[bk]], writes=[b]),
                                    g.sc["sr"][r0:r0 + 128, o:o + 512])
            flush(0)
            K.dma("sp", g.sc["qT"].rearrange("h d t -> d h t")[:, :, t0:t0 + 512], qTst[:], dqk[0], reads=[qTstb])
            K.dma("sp", g.sc["kT"].rearrange("h d t -> d h t")[:, :, t0:t0 + 512], kTst[:], dqk[1], reads=[kTstb])
        K.barrier()


def phase_attn(K, g, T):
    nc = K.nc
    NT = T // 128
    NQB = T // 512
    scale = 128.0 ** -0.5
    with ExitStack() as es:
        kT, kTb = K.sb(es, "a_kT", [128, 2, T], BF16)
        v, vb = K.sb(es, "a_v", [128, NT, 256], BF16)
        d0 = K.ds()
        K.dma("sp", kT[:], g.sc["kT"].rearrange("h d t -> d h t")[:, :, 0:T], d0, writes=[kTb])
        K.dma("sp", v[:], g.sc["v"][0:T, :].rearrange("(n p) d -> p n d", p=128), d0, writes=[vb])
        qs = [K.sb(es, "a_q%d" % i, [128, 8, 512], BF16) for i in range(2)]
        dq = [K.ds() for _ in range(2)]
        pT = [K.sb(es, "a_p%d" % i, [128, 512], BF16) for i in range(4)]
        rd, rdb = K.sb(es, "a_rd", [128, 512], F32)
        ost = [K.sb(es, "a_o%d" % i, [128, 512], BF16) for i in range(2)]
        dso = [K.ds() for _ in range(2)]
        it = 0
        for qb in range(NQB):
            q0 = qb * 512
            qt, qtb = qs[qb % 2]
            K.dma("sp", qt[:], g.sc["qT"].rearrange("h d t -> d h t")[:, :, q0:q0 + 512], dq[qb % 2], writes=[qtb])
            for h in range(8):
                kvh = h // 4
                bo = it % 2
                bd = 2 + it % 2

                def score(sc):
                    bs = 4 + sc % 4
                    K.mm(K.bank[bs][:, :], K.bankb[bs], kT[:, kvh, sc * 128:(sc + 1) * 128], qt[:, h, :], True, True, [kTb, qtb])

                score(0)
                score(1)
                for sc in range(NT):
                    if sc + 2 < NT:
                        score(sc + 2)
                    bs = 4 + sc % 4
                    p, pb = pT[sc % 4]
                    K.op("act", lambda e: e.activation(out=p[:], in_=K.bank[bs][:, :], func=AF.Exp, scale=scale, bias=g.nshift[:, 0:1]),
                         reads=[K.bankb[bs], g.cb], writes=[pb])
                    K.mm(K.bank[bo][:, :], K.bankb[bo], v[:, sc, kvh * 128:(kvh + 1) * 128], p[:], sc == 0, sc == NT - 1, [vb, pb])
                    K.mm(K.bank[bd][:, :], K.bankb[bd], g.ones_b[:], p[:], sc == 0, sc == NT - 1, [g.cb, pb])
                K.op("dve", lambda e: e.reciprocal(out=rd[:], in_=K.bank[bd][:, :]), reads=[K.bankb[bd]], writes=[rdb])
                o, ob = ost[it % 2]
                K.op("dve", lambda e: e.tensor_tensor(out=o[:], in0=K.bank[bo][:, :], in1=rd[:], op=ALU.mult), reads=[K.bankb[bo], rdb], writes=[ob])
                K.dma("sp", g.sc["attT"][h, :, q0:q0 + 512], o[:], dso[it % 2], reads=[ob])
                it += 1
        K.barrier()


def phase_gla(K, g, T, direction):
    nc = K.nc
    NB = T // 512
    U = g.tri[:, 0 + direction, :]
    Cm = g.tri[:, 2 + direction, :]
    M = g.masks[:, direction, :]
    dc = 127 if direction == 0 else 0
    wa = g.wa[direction]
    back = direction == 1
    with ExitStack() as es:
        S, Sb = K.sb(es, "g_S", [128, 4, 256], F32)
        Sh, Shb = K.sb(es, "g_Sh", [128, 4, 256], BF16)
        K.op("dve", lambda e: e.memset(S[:], 0.0), writes=[Sb])
        K.op("pool", lambda e: e.memset(Sh[:], 0.0), writes=[Shb])
        nsl = 2
        lw = [K.sb(es, "g_lw%d" % i, [17, 512], F32) for i in range(nsl)]
        for t, b in lw:
            K.op("dve", lambda e: e.memset(t[:], 1.0), writes=[b])
        gq = [K.sb(es, "g_gq%d" % i, [128, 4, 512], BF16) for i in range(nsl)]
        gkT = [K.sb(es, "g_gkT%d" % i, [128, 4, 512], BF16) for i in range(nsl)]
        gk = [K.sb(es, "g_gk%d" % i, [128, 4, 512], BF16) for i in range(nsl)]
        gv = [K.sb(es, "g_gv%d" % i, [128, 4, 1024], BF16) for i in range(nsl)]
        dld = [K.ds() for _ in range(nsl)]
        if back:
            sr = [K.sb(es, "g_sr%d" % i, [128, 4, 1024], F32) for i in range(nsl)]
            of = [K.sb(es, "g_of%d" % i, [128, 4, 1024], F32) for i in range(nsl)]
            ggb, ggbb = K.sb(es, "g_ggb", [128, 1024], F32)
            K.dma("sp", ggb[:], g.w["g_gla"][0].partition_broadcast(128), dld[0], writes=[ggbb])
            glst, glstb = K.sb(es, "g_glst", [128, 8, 512], BF16)
            dgl = K.ds()
            sqo, sqob = K.sb(es, "g_sqo", [128, 1024], F32)
            ss4, ss4b = K.sb(es, "g_ss4", [128, 8], F32)
            gl, glb = K.sb(es, "g_gl", [128, 1024], BF16)
        ab_, ab_b = K.sb(es, "g_abs", [128, 512], F32)
        ex, exb = K.sb(es, "g_ex", [128, 512], F32)
        la, lab = K.sb(es, "g_la", [128, 512], F32)
        Ec, Ecb = K.sb(es, "g_Ec", [128, 512], F32)
        kk, kkb = K.sb(es, "g_kk", [128, 512], BF16)
        Ep, Epb = K.sb(es, "g_Ep", [128, 4, 128], F32)
        Em, Emb = K.sb(es, "g_Em", [128, 4, 128], F32)
        qe, qeb = K.sb(es, "g_qe", [128, 4, 128], BF16)
        ke, keb = K.sb(es, "g_ke", [128, 4, 128], BF16)
        Am, Amb = K.sb(es, "g_Am", [128, 4, 128], BF16)
        oo = [K.sb(es, "g_oo%d" % i, [128, 1024], F32) for i in range(2)]
        doo = [K.ds() for _ in range(2)]
        blocks = list(range(NB))
        if back:
            blocks = blocks[::-1]
        ntile = 0
        for bi, blk in enumerate(blocks):
            t0 = blk * 512
            sl = bi % nsl
            lw_t, lw_b = lw[sl]
            gq_t, gq_b = gq[sl]
            gkT_t, gkT_b = gkT[sl]
            gk_t, gk_b = gk[sl]
            gv_t, gv_b = gv[sl]
            K.dma("sp", lw_t[0:16, :], g.sc["lowT"][direction, :, t0:t0 + 512], dld[sl], writes=[lw_b])
            K.dma("sp", gq_t[:], g.sc["gqT"].rearrange("h d t -> d h t")[:, :, t0:t0 + 512], dld[sl], writes=[gq_b])
            K.dma("sp", gkT_t[:], g.sc["gkT"].rearrange("h d t -> d h t")[:, :, t0:t0 + 512], dld[sl], writes=[gkT_b])
            K.dma("sp", gk_t[:], g.sc["gk"][t0:t0 + 512, :].rearrange("(i p) d -> p i d", p=128), dld[sl], writes=[gk_b])
            K.dma("sp", gv_t[:], g.sc["gv"][t0:t0 + 512, :].rearrange("(i p) d -> p i d", p=128), dld[sl], writes=[gv_b])
            if back:
                sr_t, sr_b = sr[sl]
                of_t, of_b = of[sl]
                K.dma("sp", sr_t[:], g.sc["sr"][t0:t0 + 512, :].rearrange("(i p) d -> p i d", p=128), dld[sl], writes=[sr_b])
                K.dma("sp", of_t[:], g.sc["of"][t0:t0 + 512, :].rearrange("(i p) d -> p i d", p=128), dld[sl], writes=[of_b])
            tiles = list(range(4))
            if back:
                tiles = tiles[::-1]
            for i in tiles:
                c0 = i * 128
                K.mm(K.bank[0][:, :], K.bankb[0], lw_t[0:17, c0:c0 + 128], wa[0:17, :], True, True, [lw_b, g.cb])
                K.op("act", lambda e: e.activation(out=ab_[:], in_=K.bank[0][:, :], func=AF.Abs), reads=[K.bankb[0]], writes=[ab_b])
                K.op("act", lambda e: e.activation(out=ex[:], in_=ab_[:], func=AF.Exp, scale=-1.0), reads=[ab_b], writes=[exb])
                K.op("act", lambda e: e.activation(out=ex[:], in_=ex[:], func=AF.Ln, bias=1.0), reads=[exb], writes=[exb])
                K.op("dve", lambda e: e.scalar_tensor_tensor(out=la[:], in0=K.bank[0][:, :], scalar=0.0, in1=ex[:], op0=ALU.min, op1=ALU.subtract),
                     reads=[K.bankb[0], exb], writes=[lab])
                K.mm(K.bank[0][:, :], K.bankb[0], Cm, la[:], True, True, [lab, g.cb])
                K.op("act", lambda e: e.activation(out=Ec[:], in_=K.bank[0][:, :], func=AF.Exp), reads=[K.bankb[0]], writes=[Ecb])
                K.op("dve", lambda e: e.tensor_tensor(out=kk[:], in0=gk_t[:, i, :], in1=Ec[:], op=ALU.mult), reads=[gk_b, Ecb], writes=[kkb])
                for h in range(4):
                    K.mm(K.bank[1][:, h * 128:(h + 1) * 128], K.bankb[1], la[:, h * 128:(h + 1) * 128], U, True, True, [lab, g.cb])
                K.op("act", lambda e: e.activation(out=Ep[:].rearrange("p h t -> p (h t)"), in_=K.bank[1][:, :], func=AF.Exp), reads=[K.bankb[1]], writes=[Epb])
                K.op("act", lambda e: e.activation(out=Em[:].rearrange("p h t -> p (h t)"), in_=K.bank[1][:, :], func=AF.Exp, scale=-1.0), reads=[K.bankb[1]], writes=[Emb])
                K.op("dve", lambda e: e.tensor_tensor(out=qe[:], in0=gq_t[:, :, c0:c0 + 128], in1=Ep[:], op=ALU.mult), reads=[gq_b, Epb], writes=[qeb])
                K.op("dve", lambda e: e.tensor_tensor(out=ke[:], in0=gkT_t[:, :, c0:c0 + 128], in1=Em[:], op=ALU.mult), reads=[gkT_b, Emb], writes=[keb])
                for h in range(4):
                    K.mm(K.bank[2][:, h * 128:(h + 1) * 128], K.bankb[2], ke[:, h, :], qe[:, h, :], True, True, [keb, qeb])
                K.op("dve", lambda e: e.tensor_tensor(out=Am[:], in0=K.bank[2][:, :].rearrange("p (h t) -> p h t", h=4), in1=M.unsqueeze(1).broadcast_to([128, 4, 128]), op=ALU.mult),
                     reads=[K.bankb[2], g.cb], writes=[Amb])
                for h in range(4):
                    bo = 3 + h // 2
                    bs = 5 + h // 2
                    osl = K.bank[bo][:, (h % 2) * 256:(h % 2 + 1) * 256]
                    K.mm(osl, K.bankb[bo], Am[:, h, :], gv_t[:, i, h * 256:(h + 1) * 256], True, False, [Amb, gv_b])
                    K.mm(osl, K.bankb[bo], qe[:, h, :], Sh[:, h, :], False, True, [qeb, Shb])
                    K.mm(K.bank[bs][:, (h % 2) * 256:(h % 2 + 1) * 256], K.bankb[bs], kk[:, h * 128:(h + 1) * 128], gv_t[:, i, h * 256:(h + 1) * 256], True, True, [kkb, gv_b])
                for h in range(4):
                    bs = 5 + h // 2
                    K.op("dve", lambda e: e.scalar_tensor_tensor(out=S[:, h, :], in0=S[:, h, :], scalar=Ep[:, h, dc:dc + 1], in1=K.bank[bs][:, (h % 2) * 256:(h % 2 + 1) * 256],
                                                                 op0=ALU.mult, op1=ALU.add), reads=[Sb, Epb, K.bankb[bs]], writes=[Sb])
                K.op("pool", lambda e: e.tensor_copy(out=Sh[:], in_=S[:]), reads=[Sb], writes=[Shb])
                o_t, o_b = oo[ntile % 2]
                r0 = t0 + c0
                if not back:
                    for hh in range(2):
                        K.op("act", lambda e: e.copy(out=o_t[:, hh * 512:(hh + 1) * 512], in_=K.bank[3 + hh][:, :]), reads=[K.bankb[3 + hh]], writes=[o_b])
                    K.dma("sp", g.sc["of"][r0:r0 + 128, :], o_t[:], doo[ntile % 2], reads=[o_b])
                else:
                    for hh in range(2):
                        K.op("dve", lambda e: e.tensor_tensor(out=o_t[:, hh * 512:(hh + 1) * 512], in0=K.bank[3 + hh][:, :], in1=of_t[:, i, hh * 512:(hh + 1) * 512], op=ALU.add),
                             reads=[K.bankb[3 + hh], of_b], writes=[o_b])
                    K.op("act", lambda e: e.activation(out=sqo[:], in_=o_t[:], func=AF.Square), reads=[o_b], writes=[sqob])
                    K.op("dve", lambda e: e.tensor_reduce(out=ss4[:, 0:4], in_=sqo[:].rearrange("p (h d) -> p h d", h=4), axis=AX.X, op=ALU.add), reads=[sqob], writes=[ss4b])
                    K.op("dve", lambda e: e.tensor_scalar(out=ss4[:, 0:4], in0=ss4[:, 0:4], scalar1=1.0 / 256, scalar2=EPS, op0=ALU.mult, op1=ALU.add), reads=[ss4b], writes=[ss4b])
                    K.op("act", lambda e: e.activation(out=ss4[:, 0:4], in_=ss4[:, 0:4], func=AF.Sqrt), reads=[ss4b], writes=[ss4b])
                    K.op("dve", lambda e: e.reciprocal(out=ss4[:, 4:8], in_=ss4[:, 0:4]), reads=[ss4b], writes=[ss4b])
                    o3 = o_t[:].rearrange("p (h d) -> p h d", h=4)
                    K.op("dve", lambda e: e.tensor_tensor(out=o3, in0=o3, in1=ss4[:, 4:8].unsqueeze(2).broadcast_to([128, 4, 256]), op=ALU.mult), reads=[o_b, ss4b], writes=[o_b])
                    K.op("pool", lambda e: e.tensor_tensor(out=o_t[:], in0=o_t[:], in1=ggb[:], op=ALU.mult), reads=[o_b, ggbb], writes=[o_b])
                    K.op("pool", lambda e: e.tensor_tensor(out=gl[:], in0=o_t[:], in1=sr_t[:, i, :], op=ALU.mult), reads=[o_b, sr_b], writes=[glb])
                    tb = K.bank[7][:].bitcast(BF16)
                    for j in range(8):
                        K.tr(tb[:, j * 128:(j + 1) * 128], K.bankb[7], gl[:, j * 128:(j + 1) * 128], g.ident_b[:], [glb, g.cb])
                    K.op("act", lambda e: e.copy(out=glst[:, :, c0:c0 + 128], in_=tb[:, 0:1024].rearrange("p (j t) -> p j t", j=8)), reads=[K.bankb[7]], writes=[glstb])
                ntile += 1
            if back:
                K.dma("sp", g.sc["glaT"].rearrange("j f t -> f j t")[:, :, t0:t0 + 512], glst[:], dgl, reads=[glstb])
        K.barrier()


def phase3(K, g, s, T, x_in):
    nc = K.nc
    NB = T // 512
    win, wbra, wbrg, wout = g.wbf["w_in"], g.wbf["w_br_att"], g.wbf["w_br_gla"], g.wbf["w_out"]
    with ExitStack() as es:
        gt1, gt1b = make_gate_bcast(K, g, es, s, 0, "gt1")
        hT, hTb = K.sb(es, "m_hT", [128, KC, 512], BF16)
        at, atb = K.sb(es, "m_at", [128, 8, 512], BF16)
        gl, glb = K.sb(es, "m_gl", [128, 8, 512], BF16)
        dact = [K.ds() for _ in range(3)]
        wga = [K.sb(es, "m_wga%d" % i, [128, KC, 256], BF16) for i in range(2)]
        wgg = [K.sb(es, "m_wgg%d" % i, [128, KC, 256], BF16) for i in range(2)]
        wba = [K.sb(es, "m_wba%d" % i, [128, 8, 256], BF16) for i in range(2)]
        wbg = [K.sb(es, "m_wbg%d" % i, [128, 8, 256], BF16) for i in range(2)]
        dwt = [K.ds() for _ in range(2)]
        wo = [K.sb(es, "m_wo%d" % i, [128, KC, 512], BF16) for i in range(2)]
        dwo = [K.ds() for _ in range(2)]
        mg, mgb = K.sb(es, "m_mg", [128, KC, 512], BF16)
        sga = [K.sb(es, "m_sga%d" % i, [128, 512], F32) for i in range(2)]
        sgg = [K.sb(es, "m_sgg%d" % i, [128, 512], F32) for i in range(2)]
        tmp, tmpb = K.sb(es, "m_tmp", [128, 512], F32)
        tmp2, tmp2b = K.sb(es, "m_tmp2", [128, 512], F32)
        xblk = es.enter_context(nc.sbuf_tensor(K.nm("m_xblk"), [128, 4, D], F32))
        xb = [Buf() for _ in range(4)]
        dx = [K.ds() for _ in range(4)]
        dxs = [K.ds() for _ in range(4)]
        cnt = dict(wt=0, wo=0, ch=0, mb=0)

        def load_gate_w(gi):
            j = cnt["wt"] % 2
            cnt["wt"] += 1
            K.dma("sp", wga[j][0][:], kc_view(win, C_GA + gi * 256, C_GA + (gi + 1) * 256), dwt[j], writes=[wga[j][1]])
            K.dma("sp", wgg[j][0][:], kc_view(win, C_GG + gi * 256, C_GG + (gi + 1) * 256), dwt[j], writes=[wgg[j][1]])
            K.dma("sp", wba[j][0][:], kc_view(wbra, gi * 256, (gi + 1) * 256), dwt[j], writes=[wba[j][1]])
            K.dma("sp", wbg[j][0][:], kc_view(wbrg, gi * 256, (gi + 1) * 256), dwt[j], writes=[wbg[j][1]])
            return j

        def load_wo(cg):
            j = cnt["wo"] % 2
            cnt["wo"] += 1
            K.dma("sp", wo[j][0][:], kc_view(wout, cg * 512, (cg + 1) * 512), dwo[j], writes=[wo[j][1]])
            return j

        def load_acts(blk_):
            t0_ = blk_ * 512
            K.dma("sp", hT[:], g.sc["hT"].rearrange("c p t -> p c t")[:, :, t0_:t0_ + 512], dact[0], writes=[hTb])
            K.dma("sp", at[:], g.sc["attT"].rearrange("c p t -> p c t")[:, :, t0_:t0_ + 512], dact[1], writes=[atb])
            K.dma("sp", gl[:], g.sc["glaT"].rearrange("c p t -> p c t")[:, :, t0_:t0_ + 512], dact[2], writes=[glb])

        load_acts(0)
        wq = [load_gate_w(0)]
        for blk in range(NB):
            t0 = blk * 512
            for gi in range(8):
                j = wq.pop(0)
                if gi + 1 < 8:
                    wq.append(load_gate_w(gi + 1))
                if gi == 4:
                    for i in range(4):
                        K.dma("sp", xblk[:, i, :], x_in[t0 + i * 128:t0 + (i + 1) * 128, :], dx[i], writes=[xb[i]])
                for cc in range(2):
                    ch = gi * 2 + cc
                    b0 = (cnt["ch"] % 2) * 4
                    sa, sab = sga[cnt["ch"] % 2]
                    sg_, sgb = sgg[cnt["ch"] % 2]
                    cnt["ch"] += 1
                    cs_ = slice(cc * 128, (cc + 1) * 128)
                    for kc in range(KC):
                        K.mm(K.bank[b0][:, :], K.bankb[b0], wga[j][0][:, kc, cs_], hT[:, kc, :], kc == 0, kc == KC - 1, [wga[j][1], hTb])
                    for kc in range(KC):
                        K.mm(K.bank[b0 + 1][:, :], K.bankb[b0 + 1], wgg[j][0][:, kc, cs_], hT[:, kc, :], kc == 0, kc == KC - 1, [wgg[j][1], hTb])
                    for kc in range(8):
                        K.mm(K.bank[b0 + 2][:, :], K.bankb[b0 + 2], wba[j][0][:, kc, cs_], at[:, kc, :], kc == 0, kc == 7, [wba[j][1], atb])
                    for kc in range(8):
                        K.mm(K.bank[b0 + 3][:, :], K.bankb[b0 + 3], wbg[j][0][:, kc, cs_], gl[:, kc, :], kc == 0, kc == 7, [wbg[j][1], glb])
                    K.op("act", lambda e: e.activation(out=sa[:], in_=K.bank[b0][:, :], func=AF.Sigmoid), reads=[K.bankb[b0]], writes=[sab])
                    K.op("act", lambda e: e.activation(out=sg_[:], in_=K.bank[b0 + 1][:, :], func=AF.Sigmoid), reads=[K.bankb[b0 + 1]], writes=[sgb])
                    K.op("dve", lambda e: e.tensor_tensor(out=tmp[:], in0=K.bank[b0 + 2][:, :], in1=sa[:], op=ALU.mult), reads=[K.bankb[b0 + 2], sab], writes=[tmpb])
                    K.op("dve", lambda e: e.tensor_tensor(out=tmp2[:], in0=K.bank[b0 + 3][:, :], in1=sg_[:], op=ALU.mult), reads=[K.bankb[b0 + 3], sgb], writes=[tmp2b])
                    K.op("pool", lambda e: e.tensor_tensor(out=mg[:, ch, :], in0=tmp[:], in1=tmp2[:], op=ALU.add), reads=[tmpb, tmp2b], writes=[mgb])
            wqo = [load_wo(0)]
            if blk + 1 < NB:
                load_acts(blk + 1)
                wq = [load_gate_w(0)]
            for cg in range(4):
                j = wqo.pop(0)
                if cg + 1 < 4:
                    wqo.append(load_wo(cg + 1))
                cols = slice(cg * 512, (cg + 1) * 512)
                for i in range(4):
                    bk = cnt["mb"] % 4
                    cnt["mb"] += 1
                    for kc in range(KC):
                        K.mm(K.bank[bk][:, :], K.bankb[bk], mg[:, kc, i * 128:(i + 1) * 128], wo[j][0][:, kc, :], kc == 0, kc == KC - 1, [mgb, wo[j][1]])
                    K.op("dve", lambda e: e.tensor_tensor(out=tmp[:], in0=K.bank[bk][:, :], in1=gt1[:, cols], op=ALU.mult), reads=[K.bankb[bk], gt1b], writes=[tmpb])
                    K.op("pool", lambda e: e.tensor_tensor(out=xblk[:, i, cols], in0=xblk[:, i, cols], in1=tmp[:], op=ALU.add), reads=[xb[i], tmpb], writes=[xb[i]])
            for i in range(4):
                K.dma("sp", g.sc["x1"][t0 + i * 128:t0 + (i + 1) * 128, :], xblk[:, i, :], dxs[i], reads=[xb[i]])
        K.barrier()


def phase4a(K, g, T):
    nc = K.nc
    wup = g.wbf["w_up"]
    nW = -(-T // 510)
    bounds = [int(round(k * T / nW)) for k in range(nW + 1)]
    W0, W1, W2, WB = 128, 216, 304, 392
    with ExitStack() as es:
        hw = [K.sb(es, "f_h%d" % i, [128, KC, 512], BF16) for i in range(2)]
        dh = [K.ds() for _ in range(2)]
        wv = [K.sb(es, "f_wv%d" % i, [128, KC, 512], BF16) for i in range(2)]
        wg = [K.sb(es, "f_wg%d" % i, [128, KC, 512], BF16) for i in range(2)]
        dw = [K.ds() for _ in range(2)]
        cv = [K.sb(es, "f_cv%d" % i, [128, 512], F32) for i in range(2)]
        cg_ = [K.sb(es, "f_cg%d" % i, [128, 512], F32) for i in range(2)]
        ast = [K.sb(es, "f_ast%d" % i, [128, 4, 512], BF16) for i in range(2)]
        dast = [K.ds() for _ in range(2)]
        cnt = dict(w=0, ch=0, st=0)

        def load_w(gi):
            j = cnt["w"] % 2
            cnt["w"] += 1
            K.dma("sp", wv[j][0][:], kc_view(wup, gi * 512, (gi + 1) * 512), dw[j], writes=[wv[j][1]])
            K.dma("sp", wg[j][0][:], kc_view(wup, DFF + gi * 512, DFF + (gi + 1) * 512), dw[j], writes=[wg[j][1]])
            return j

        def conv(bk, ch, out, outb, Wu):
            vt = g.vecT
            K.op("act", lambda e: e.activation(out=out[:, 0:Wu], in_=K.bank[bk][:, 0:Wu], func=AF.Identity, scale=vt[:, W1 + ch:W1 + ch + 1], bias=vt[:, WB + ch:WB + ch + 1]),
                 reads=[K.bankb[bk], g.cb], writes=[outb])
            K.op("dve", lambda e: e.scalar_tensor_tensor(out=out[:, 1:Wu], in0=K.bank[bk][:, 0:Wu - 1], scalar=vt[:, W0 + ch:W0 + ch + 1], in1=out[:, 1:Wu], op0=ALU.mult, op1=ALU.add),
                 reads=[K.bankb[bk], g.cb, outb], writes=[outb])
            K.op("dve", lambda e: e.scalar_tensor_tensor(out=out[:, 0:Wu - 1], in0=K.bank[bk][:, 1:Wu], scalar=vt[:, W2 + ch:W2 + ch + 1], in1=out[:, 0:Wu - 1], op0=ALU.mult, op1=ALU.add),
                 reads=[K.bankb[bk], g.cb, outb], writes=[outb])

        wq = [load_w(0)]
        for wi in range(nW):
            olo, ohi = bounds[wi], bounds[wi + 1]
            a, b = max(olo - 1, 0), min(ohi + 1, T)
            Wu, lo, hi = b - a, olo - a, ohi - a
            h, hb = hw[wi % 2]
            K.dma("sp", h[:, :, 0:Wu], g.sc["h2T"].rearrange("c p t -> p c t")[:, :, a:b], dh[wi % 2], writes=[hb])
            for gi in range(11):
                j = wq.pop(0)
                if gi + 1 < 11:
                    wq.append(load_w(gi + 1))
                elif wi + 1 < nW:
                    wq.append(load_w(0))
                st, stb_ = ast[cnt["st"] % 2]
                dst_ = dast[cnt["st"] % 2]
                cnt["st"] += 1
                for jj in range(4):
                    ch = gi * 4 + jj
                    k2 = cnt["ch"] % 2
                    cnt["ch"] += 1
                    bv, bg = k2 * 2, k2 * 2 + 1
                    cs_ = slice(jj * 128, (jj + 1) * 128)
                    for kc in range(KC):
                        K.mm(K.bank[bv][:, 0:Wu], K.bankb[bv], wv[j][0][:, kc, cs_], h[:, kc, 0:Wu], kc == 0, kc == KC - 1, [wv[j][1], hb])
                    for kc in range(KC):
                        K.mm(K.bank[bg][:, 0:Wu], K.bankb[bg], wg[j][0][:, kc, cs_], h[:, kc, 0:Wu], kc == 0, kc == KC - 1, [wg[j][1], hb])
                    cvt, cvb = cv[k2]
                    cgt, cgb = cg_[k2]
                    conv(bv, ch, cvt, cvb, Wu)
                    conv(bg, NFC + ch, cgt, cgb, Wu)
                    K.op("act", lambda e: e.activation(out=cgt[:, 0:Wu], in_=cgt[:, 0:Wu], func=AF.Silu), reads=[cgb], writes=[cgb])
                    K.op("pool", lambda e: e.tensor_tensor(out=st[:, jj, 0:Wu], in0=cgt[:, 0:Wu], in1=cvt[:, 0:Wu], op=ALU.mult), reads=[cgb, cvb], writes=[stb_])
                K.dma("sp", g.sc["actT"].rearrange("c p t -> p c t")[:, gi * 4:(gi + 1) * 4, olo:ohi], st[:, :, lo:hi], dst_, reads=[stb_])
        K.barrier()


def phase4b(K, g, s, T, y_out):
    nc = K.nc
    NB = T // 512
    wdn = g.wbf["w_down"]
    with ExitStack() as es:
        gt2, gt2b = make_gate_bcast(K, g, es, s, 1, "gt2")
        gfin, gfinb = K.sb(es, "d_gfin", [128, D], F32)
        d0 = K.ds()
        K.dma("sp", gfin[:], g.w["g_final"].partition_broadcast(128), d0, writes=[gfinb])
        act, actb = K.sb(es, "d_act", [128, NFC, 512], BF16)
        dact = K.ds()
        wd = [K.sb(es, "d_wd%d" % i, [128, NFC, 512], BF16) for i in range(2)]
        dwd = [K.ds() for _ in range(2)]
        xblk = es.enter_context(nc.sbuf_tensor(K.nm("d_xblk"), [128, 4, D], F32))
        xb = [Buf() for _ in range(4)]
        dx = [K.ds() for _ in range(4)]
        tmp = [K.sb(es, "d_tmp%d" % i, [128, 512], F32) for i in range(2)]
        ssq, ssqb = K.sb(es, "d_ssq", [128, 8], F32)
        ot = [K.sb(es, "d_ot%d" % i, [128, D], F32) for i in range(1)]
        dot = [K.ds() for _ in range(1)]
        junk, junkb = ot[0]
        cnt = dict(w=0, mb=0, ot=0)

        def load_wd(cg):
            j = cnt["w"] % 2
            cnt["w"] += 1
            K.dma("sp", wd[j][0][:], kc_view(wdn, cg * 512, (cg + 1) * 512), dwd[j], writes=[wd[j][1]])
            return j

        wq = [load_wd(0)]
        for blk in range(NB):
            t0 = blk * 512
            K.dma("sp", act[:], g.sc["actT"].rearrange("c p t -> p c t")[:, :, t0:t0 + 512], dact, writes=[actb])
            for i in range(4):
                K.dma("sp", xblk[:, i, :], g.sc["x1"][t0 + i * 128:t0 + (i + 1) * 128, :], dx[i], writes=[xb[i]])
            for cg in range(4):
                j = wq.pop(0)
                if cg + 1 < 4:
                    wq.append(load_wd(cg + 1))
                elif blk + 1 < NB:
                    wq.append(load_wd(0))
                cols = slice(cg * 512, (cg + 1) * 512)
                for i in range(4):
                    bk = cnt["mb"] % 4
                    tt, ttb = tmp[cnt["mb"] % 2]
                    cnt["mb"] += 1
                    for kc in range(NFC):
                        K.mm(K.bank[bk][:, :], K.bankb[bk], act[:, kc, i * 128:(i + 1) * 128], wd[j][0][:, kc, :], kc == 0, kc == NFC - 1, [actb, wd[j][1]])
                    K.op("dve", lambda e: e.tensor_tensor(out=tt[:], in0=K.bank[bk][:, :], in1=gt2[:, cols], op=ALU.mult), reads=[K.bankb[bk], gt2b], writes=[ttb])
                    K.op("pool", lambda e: e.tensor_tensor(out=xblk[:, i, cols], in0=xblk[:, i, cols], in1=tt[:], op=ALU.add), reads=[xb[i], ttb], writes=[xb[i]])
            for i in range(4):
                K.op("act", lambda e: e.activation(out=junk[:], in_=xblk[:, i, :], func=AF.Square, accum_out=ssq[:, i:i + 1]), reads=[xb[i]], writes=[junkb, ssqb])
            K.op("dve", lambda e: e.tensor_scalar(out=ssq[:, 0:4], in0=ssq[:, 0:4], scalar1=1.0 / D, scalar2=EPS, op0=ALU.mult, op1=ALU.add), reads=[ssqb], writes=[ssqb])
            K.op("act", lambda e: e.activation(out=ssq[:, 0:4], in_=ssq[:, 0:4], func=AF.Sqrt), reads=[ssqb], writes=[ssqb])
            K.op("dve", lambda e: e.reciprocal(out=ssq[:, 4:8], in_=ssq[:, 0:4]), reads=[ssqb], writes=[ssqb])
            for i in range(4):
                o, ob = ot[0]
                dd = dot[0]
                cnt["ot"] += 1
                K.op("dve", lambda e: e.scalar_tensor_tensor(out=o[:], in0=xblk[:, i, :], scalar=ssq[:, 4 + i:5 + i], in1=gfin[:], op0=ALU.mult, op1=ALU.mult),
                     reads=[xb[i], ssqb, gfinb], writes=[ob])
                K.dma("sp", y_out[t0 + i * 128:t0 + (i + 1) * 128, :], o[:], dd, reads=[ob])
        K.barrier()


W_SHAPES = {
    "w_mod": (1, D, 6 * D), "b_mod": (1, 6 * D), "g_mix_norm": (1, D), "w_in": (1, D, IN_COLS), "g_q": (1, 128), "g_k": (1, 128),
    "w_a_up_f": (1, 16, 512), "b_a_f": (1, 512), "w_a_up_b": (1, 16, 512), "b_a_b": (1, 512), "g_gla": (1, 1024),
    "w_br_att": (1, 1024, D), "w_br_gla": (1, 1024, D), "w_out": (1, D, D), "g_ffn_norm": (1, D), "w_up": (1, D, 2 * DFF),
    "w_conv": (1, 3, 2 * DFF), "b_conv": (1, 2 * DFF), "w_down": (1, DFF, D), "g_final": (D,),
}
CONV_W = [("w_in", D, IN_COLS), ("w_br_att", 1024, D), ("w_br_gla", 1024, D), ("w_out", D, D), ("w_up", D, 2 * DFF), ("w_down", DFF, D)]


def build(seq_T, debug=False, phases=None):
    nc = bass.Bass("TRN2", target_bir_lowering=False)
    g = G()
    Tmax = max(seq_T)

    def dt(name, shape, dtype, kind):
        return nc.dram_tensor(name, list(shape), dtype, kind=kind).ap()

    g.w = {n: dt(n, sh, F32, "ExternalInput") for n, sh in W_SHAPES.items()}
    xs = [dt("xin%d" % i, [T, D], F32, "ExternalInput") for i, T in enumerate(seq_T)]
    g.d_c2 = dt("c2", [2, D], F32, "ExternalInput")
    g.d_ident = dt("k_ident", [128, 128], F32, "ExternalInput")
    g.d_masks = dt("k_masks", [2, 128, 128], F32, "ExternalInput")
    g.d_tri = dt("k_tri", [4, 128, 128], F32, "ExternalInput")
    g.d_cos = dt("k_cos", [Tmax, 128], F32, "ExternalInput")
    g.d_sin = dt("k_sin", [Tmax, 128], F32, "ExternalInput")
    ys = [dt("yout%d" % i, [T, D], F32, "ExternalOutput") for i, T in enumerate(seq_T)]
    sk = "ExternalOutput" if debug else "Internal"
    g.wbf = {n: dt("bf_" + n, [Kd, N], BF16, "Internal") for n, Kd, N in CONV_W}
    g.conv_pairs = [(g.w[n][0], g.wbf[n], Kd, N) for n, Kd, N in CONV_W]
    g.sc = {
        "qT": dt("s_qT", [8, 128, Tmax], BF16, sk), "kT": dt("s_kT", [2, 128, Tmax], BF16, sk), "v": dt("s_v", [Tmax, 256], BF16, sk),
        "gqT": dt("s_gqT", [4, 128, Tmax], BF16, sk), "gkT": dt("s_gkT", [4, 128, Tmax], BF16, sk), "gk": dt("s_gk", [Tmax, 512], BF16, sk),
        "gv": dt("s_gv", [Tmax, 1024], BF16, sk), "sr": dt("s_sr", [Tmax, 1024], F32, sk), "lowT": dt("s_lowT", [2, 16, Tmax], F32, sk),
        "hT": dt("s_hT", [KC, 128, Tmax], BF16, sk), "attT": dt("s_attT", [8, 128, Tmax], BF16, sk), "of": dt("s_of", [Tmax, 1024], F32, sk),
        "glaT": dt("s_glaT", [8, 128, Tmax], BF16, sk), "x1": dt("s_x1", [Tmax, D], F32, sk), "h2T": dt("s_h2T", [KC, 128, Tmax], BF16, sk),
        "actT": dt("s_actT", [NFC, 128, Tmax], BF16, sk),
    }
    with ExitStack() as es:
        K = Ctx(nc, es)
        phase_convert(K, g)
        phase_setup(K, g, es)
        for s, T in enumerate(seq_T):
            plist = [("p0", lambda: phase_prep(K, g, s, T, xs[s], g.sc["hT"], 0)), ("p1", lambda: phase1(K, g, s, T, xs[s])), ("attn", lambda: phase_attn(K, g, T)), ("glaf", lambda: phase_gla(K, g, T, 0)),
                     ("glab", lambda: phase_gla(K, g, T, 1)), ("p3", lambda: phase3(K, g, s, T, xs[s])),
                     ("p3b", lambda: phase_prep(K, g, s, T, g.sc["x1"], g.sc["h2T"], 2)), ("p4a", lambda: phase4a(K, g, T)),
                     ("p4b", lambda: phase4b(K, g, s, T, ys[s]))]
            for name, fn in plist:
                if phases is None or name in phases:
                    fn()
        K.barrier()
    return nc


def make_consts(Tmax):
    idx = np.arange(128)
    s_le_t = (idx[:, None] <= idx[None, :]).astype(np.float32)
    s_ge_t = (idx[:, None] >= idx[None, :]).astype(np.float32)
    s_gt_t = (idx[:, None] > idx[None, :]).astype(np.float32)
    s_lt_t = (idx[:, None] < idx[None, :]).astype(np.float32)
    tri = np.stack([s_le_t, s_ge_t, s_gt_t, s_lt_t]) / 16.0
    masks = np.stack([s_le_t, s_gt_t])
    t = np.arange(Tmax)
    row = (t // 64).astype(np.float32)
    col = (t % 64).astype(np.float32)
    inv = (np.float32(10000.0) ** (-np.arange(0, 64, 2, dtype=np.float32) / np.float32(64))).astype(np.float32)
    ar = row[:, None] * inv
    ac = col[:, None] * inv
    ang = np.concatenate([ar, ar, ac, ac], axis=-1).astype(np.float32)
    cos = np.cos(ang).astype(np.float32)
    sin = np.sin(ang).astype(np.float32)
    sgn = np.concatenate([-np.ones(32), np.ones(32), -np.ones(32), np.ones(32)]).astype(np.float32)
    return {"k_ident": np.eye(128, dtype=np.float32), "k_masks": masks.astype(np.float32), "k_tri": tri.astype(np.float32),
            "k_cos": cos, "k_sin": (sin * sgn[None, :]).astype(np.float32)}


_NC_CACHE = {}


def kernel(**inputs):
    seq_T = [inputs["x_prompt"].shape[1], inputs["x_sample"].shape[1]]
    key = tuple(seq_T)
    if key not in _NC_CACHE:
        _NC_CACHE[key] = build(seq_T)
    nc = _NC_CACHE[key]
    consts = make_consts(max(seq_T))
    wts = {n: np.ascontiguousarray(np.asarray(inputs[n], dtype=np.float32)) for n in W_SHAPES}
    in_maps = []
    for i in range(N_CORES):
        m = dict(wts)
        m.update(consts)
        m["xin0"] = np.ascontiguousarray(np.asarray(inputs["x_prompt"][i], dtype=np.float32))
        m["xin1"] = np.ascontiguousarray(np.asarray(inputs["x_sample"][i], dtype=np.float32))
        m["c2"] = np.ascontiguousarray(np.stack([np.asarray(inputs["c_prompt"][i]), np.asarray(inputs["c_sample"][i])]).astype(np.float32))
        in_maps.append(m)
    res = run_bass_kernel_spmd(nc, in_maps, core_ids=list(range(N_CORES)))
    y_p = np.stack([np.asarray(r["yout0"], dtype=np.float32) for r in res.results])
    y_s = np.stack([np.asarray(r["yout1"], dtype=np.float32) for r in res.results])
    return (y_p, y_s)
```

```python
import numpy as np
from contextlib import ExitStack
import concourse.bass as bass
import concourse.mybir as mybir
from concourse.bass_utils import run_bass_kernel_spmd
from concourse.alu_op_type import AluOpType as ALU

F32 = mybir.dt.float32
BF16 = mybir.dt.bfloat16
AF = mybir.ActivationFunctionType
AX = mybir.AxisListType

D = 2048
KC = 16
IN_COLS = 8736
DFF = 5632
NFC = 44
EPS = 1e-6
C_AQ, C_AK, C_AV, C_GQ, C_GK, C_GV, C_GR, C_LOW, C_GA, C_GG = 0, 1024, 1280, 1536, 2048, 2560, 3584, 4608, 4640, 6688
N_CORES = 8


class Buf:
    __slots__ = ("w", "r")

    def __init__(self):
        self.w = None
        self.r = {}


class DS:
    def __init__(self, sem, name):
        self.sem = sem
        self.n = 0
        self.name = name


class Ctx:
    def __init__(self, nc, es):
        self.nc = nc
        self.es = es
        self.E = dict(pe=nc.tensor, act=nc.scalar, dve=nc.vector, pool=nc.gpsimd, sp=nc.sync)
        self.csem = {e: es.enter_context(nc.semaphore("cs_" + e)) for e in ("pe", "act", "dve", "pool")}
        self.cnt = dict(pe=0, act=0, dve=0, pool=0)
        self.waited = {e: {} for e in self.E}
        self.ds_pool = [DS(es.enter_context(nc.semaphore("ds%d" % i)), "ds%d" % i) for i in range(72)]
        self.ds_next = 0
        self.bank = []
        self.bankb = []
        for i in range(8):
            self.bank.append(es.enter_context(nc.psum_tensor("bank%d" % i, [128, 512], F32)))
            self.bankb.append(Buf())

    def ds(self):
        d = self.ds_pool[self.ds_next % len(self.ds_pool)]
        self.ds_next += 1
        return d

    def nm(self, name):
        self.uid = getattr(self, "uid", 0) + 1
        return "%s_u%d" % (name, self.uid)

    def sb(self, es, name, shape, dtype):
        t = es.enter_context(self.nc.sbuf_tensor(self.nm(name), list(shape), dtype))
        return t, Buf()

    def _collect(self, reads, writes):
        deps = {}

        def add(tok):
            k = tok[0]
            if k not in deps or deps[k][1] < tok[2]:
                deps[k] = (tok[1], tok[2], tok[3])

        for b in reads:
            if b.w is not None:
                add(b.w)
        for b in writes:
            if b.w is not None:
                add(b.w)
            for tok in b.r.values():
                add(tok)
        return deps

    def _wait(self, e, deps):
        w = self.waited[e]
        eng = self.E[e]
        for k, (sem, val, src) in deps.items():
            if src == "pe" and e == "pe":
                continue
            if w.get(k, 0) >= val:
                continue
            eng.wait_ge(sem, val)
            w[k] = val

    def _record(self, tok, reads, writes):
        for b in reads:
            b.r[tok[0]] = tok
        for b in writes:
            b.w = tok
            b.r = {}

    def op(self, e, fn, reads=(), writes=()):
        self._wait(e, self._collect(reads, writes))
        inst = fn(self.E[e])
        self.cnt[e] += 1
        inst.then_inc(self.csem[e], 1)
        self._record(("c_" + e, self.csem[e], self.cnt[e], e), reads, writes)

    def dma(self, q, out, in_, ds, reads=(), writes=()):
        self._wait(q, self._collect(reads, writes))
        ds.n += 1
        self.E[q].dma_start(out=out, in_=in_).then_inc(ds.sem, 16)
        self._record(("d_" + ds.name, ds.sem, 16 * ds.n, "dma"), reads, writes)

    def barrier(self):
        deps = {}
        for e in self.cnt:
            if self.cnt[e]:
                deps["c_" + e] = (self.csem[e], self.cnt[e], e)
        for d in self.ds_pool:
            if d.n:
                deps["d_" + d.name] = (d.sem, 16 * d.n, "dma")
        for e in self.E:
            self._wait(e, deps)

    def mm(self, out, ob, lhsT, rhs, start, stop, reads):
        self.op("pe", lambda e: e.matmul(out, lhsT=lhsT, rhs=rhs, start=start, stop=stop), reads=reads, writes=[ob])

    def tr(self, out, ob, in_, ident, reads):
        self.op("pe", lambda e: e.transpose(out=out, in_=in_, identity=ident), reads=reads, writes=[ob])


def kc_view(w, c0, c1):
    return w.rearrange("(kc p) n -> p kc n", p=128)[:, :, c0:c1]


class G:
    pass


def prep_stage1(K, g, xblk, xb, wk):
    junk, junkb, ssq, ssqb = wk
    for i in range(4):
        K.op("act", lambda e: e.activation(out=junk[:], in_=xblk[:, i, :], func=AF.Square, accum_out=ssq[:, i:i + 1]),
             reads=[xb[i]], writes=[junkb, ssqb])
    K.op("dve", lambda e: e.tensor_scalar(out=ssq[:, 0:4], in0=ssq[:, 0:4], scalar1=1.0 / D, scalar2=EPS, op0=ALU.mult, op1=ALU.add),
         reads=[ssqb], writes=[ssqb])
    K.op("act", lambda e: e.activation(out=ssq[:, 0:4], in_=ssq[:, 0:4], func=AF.Sqrt), reads=[ssqb], writes=[ssqb])
    K.op("dve", lambda e: e.reciprocal(out=ssq[:, 4:8], in_=ssq[:, 0:4]), reads=[ssqb], writes=[ssqb])
    for i in range(4):
        K.op("dve", lambda e: e.tensor_scalar(out=xblk[:, i, :], in0=xblk[:, i, :], scalar1=ssq[:, 4 + i:5 + i], scalar2=None, op0=ALU.mult),
             reads=[xb[i], ssqb], writes=[xb[i]])


def prep_stage2(K, g, xblk, xb, A, B, ab, hT, hTb):
    for c in range(KC):
        bk = c % 4
        for i in range(4):
            K.tr(K.bank[bk][:, i * 128:(i + 1) * 128], K.bankb[bk], xblk[:, i, c * 128:(c + 1) * 128], g.ident_f[:], [xb[i], g.cb])
        K.op("act", lambda e: e.activation(out=hT[:, c, :], in_=K.bank[bk][:, :], func=AF.Identity, scale=A[:, c:c + 1], bias=B[:, c:c + 1]),
             reads=[K.bankb[bk], ab], writes=[hTb])


def phase_prep(K, g, s, T, src, dst, aidx):
    nc = K.nc
    NB = T // 512
    with ExitStack() as es:
        xblks = [es.enter_context(nc.sbuf_tensor(K.nm("p_xblk"), [128, 4, D], F32)) for _ in range(2)]
        xbs = [[Buf() for _ in range(4)] for _ in range(2)]
        dx = [[K.ds() for _ in range(4)] for _ in range(2)]
        hTs = [K.sb(es, "p_hT", [128, KC, 512], BF16) for _ in range(2)]
        dh = [K.ds() for _ in range(2)]
        wk = [K.sb(es, "p_junk", [128, D], BF16) + K.sb(es, "p_ssq", [128, 8], F32) for _ in range(2)]

        def loads(blk):
            for i in range(4):
                K.dma("sp", xblks[blk % 2][:, i, :], src[blk * 512 + i * 128:blk * 512 + (i + 1) * 128, :], dx[blk % 2][i], writes=[xbs[blk % 2][i]])

        loads(0)
        if NB > 1:
            loads(1)
        prep_stage1(K, g, xblks[0], xbs[0], wk[0])
        for blk in range(NB):
            t0 = blk * 512
            j = blk % 2
            if blk + 1 < NB:
                prep_stage1(K, g, xblks[1 - j], xbs[1 - j], wk[1 - j])
            hT, hTb = hTs[j]
            prep_stage2(K, g, xblks[j], xbs[j], g.AB[:, s, aidx, :], g.AB[:, s, aidx + 1, :], g.cb, hT, hTb)
            K.dma("sp", dst.rearrange("c p t -> p c t")[:, :, t0:t0 + 512], hT[:], dh[j], reads=[hTb])
            if blk + 2 < NB:
                loads(blk + 2)
        K.barrier()


def phase_convert(K, g):
    with ExitStack() as es:
        stg = [K.sb(es, "cv%d" % i, [128, 11264], BF16) for i in range(3)]
        dl = [K.ds() for _ in range(3)]
        dst_ = [K.ds() for _ in range(3)]
        i = 0
        for src, dst, Kd, N in g.conv_pairs:
            for kc in range(Kd // 128):
                t, b = stg[i % 3]
                K.dma("pool", t[:, 0:N], src[kc * 128:(kc + 1) * 128, :], dl[i % 3], writes=[b])
                K.dma("sp", dst[kc * 128:(kc + 1) * 128, :], t[:, 0:N], dst_[i % 3], reads=[b])
                i += 1
        K.barrier()


def phase_setup(K, g, es):
    nc = K.nc
    g.cb = Buf()
    d0 = K.ds()

    def ld(name, shape, dtype, src, q="sp"):
        t = es.enter_context(nc.sbuf_tensor(name, list(shape), dtype))
        K.dma(q, t[:], src, d0, writes=[g.cb])
        return t

    g.ident_f = ld("ident_f", [128, 128], F32, g.d_ident[:, :])
    g.masks = ld("masks", [128, 2, 128], F32, g.d_masks.rearrange("m p t -> p m t"))
    g.tri = ld("tri", [128, 4, 128], F32, g.d_tri.rearrange("m p t -> p m t"))
    g.gq_b = ld("gq_b", [128, 128], F32, g.w["g_q"][0].partition_broadcast(128))
    g.gk_b = ld("gk_b", [128, 128], F32, g.w["g_k"][0].partition_broadcast(128))
    g.wa = []
    for nm, wn, bn in (("f", "w_a_up_f", "b_a_f"), ("b", "w_a_up_b", "b_a_b")):
        t = es.enter_context(nc.sbuf_tensor("wa_" + nm, [17, 512], F32))
        K.dma("sp", t[0:16, :], g.w[wn][0], d0, writes=[g.cb])
        K.dma("sp", t[16:17, :], g.w[bn][0:1, :], d0, writes=[g.cb])
        g.wa.append(t)
    g.ident_b = es.enter_context(nc.sbuf_tensor("ident_b", [128, 128], BF16))
    g.ones_b = es.enter_context(nc.sbuf_tensor("ones_b", [128, 128], BF16))
    K.op("dve", lambda e: e.tensor_copy(out=g.ident_b[:], in_=g.ident_f[:]), reads=[g.cb], writes=[g.cb])
    K.op("dve", lambda e: e.memset(g.ones_b[:], 1.0), writes=[g.cb])
    g.nshift = es.enter_context(nc.sbuf_tensor("nshift", [128, 2], F32))
    K.op("dve", lambda e: e.tensor_reduce(out=g.nshift[:, 0:1], in_=g.gq_b[:], axis=AX.X, op=ALU.max, apply_absolute_value=True), reads=[g.cb], writes=[g.cb])
    K.op("dve", lambda e: e.tensor_reduce(out=g.nshift[:, 1:2], in_=g.gk_b[:], axis=AX.X, op=ALU.max, apply_absolute_value=True), reads=[g.cb], writes=[g.cb])
    K.op("dve", lambda e: e.tensor_tensor(out=g.nshift[:, 0:1], in0=g.nshift[:, 0:1], in1=g.nshift[:, 1:2], op=ALU.mult), reads=[g.cb], writes=[g.cb])
    K.op("dve", lambda e: e.tensor_scalar(out=g.nshift[:, 0:1], in0=g.nshift[:, 0:1], scalar1=-(128.0 ** 0.5), scalar2=None, op0=ALU.mult), reads=[g.cb], writes=[g.cb])

    g.vecT = es.enter_context(nc.sbuf_tensor("vecT", [128, 96 + 16 + 16 + 4 * 88], F32))
    g.modT = es.enter_context(nc.sbuf_tensor("modT", [128, 2, 96], F32))
    g.AB = es.enter_context(nc.sbuf_tensor("AB", [128, 2, 4, 16], F32))
    with ExitStack() as es2:
        rows = es2.enter_context(nc.sbuf_tensor("rows", [96, 7, 128], F32))
        rb = Buf()
        K.dma("sp", rows[0:96, 0, :], g.w["b_mod"][0].rearrange("(c p) -> c p", p=128), d0, writes=[rb])
        K.dma("sp", rows[0:16, 1, :], g.w["g_mix_norm"][0].rearrange("(c p) -> c p", p=128), d0, writes=[rb])
        K.dma("sp", rows[0:16, 2, :], g.w["g_ffn_norm"][0].rearrange("(c p) -> c p", p=128), d0, writes=[rb])
        for r in range(3):
            K.dma("sp", rows[0:88, 3 + r, :], g.w["w_conv"][0, r].rearrange("(c p) -> c p", p=128), d0, writes=[rb])
        K.dma("sp", rows[0:88, 6, :], g.w["b_conv"][0].rearrange("(c p) -> c p", p=128), d0, writes=[rb])
        specs = [(0, 96, 0), (1, 16, 96), (2, 16, 112), (3, 88, 128), (4, 88, 216), (5, 88, 304), (6, 88, 392)]
        for r, n, off in specs:
            K.tr(K.bank[0][:, off % 512:off % 512 + n], K.bankb[0], rows[0:n, r, :], g.ident_f[0:n, 0:n], [rb, g.cb])
        K.op("dve", lambda e: e.tensor_copy(out=g.vecT[:, 0:480], in_=K.bank[0][:, 0:480]), reads=[K.bankb[0]], writes=[g.cb])
        crow = es2.enter_context(nc.sbuf_tensor("crow", [16, 2, 128], F32))
        K.dma("sp", crow[:], g.d_c2.rearrange("s (kc p) -> kc s p", p=128), d0, writes=[rb])
        g_scT = es2.enter_context(nc.sbuf_tensor("scT", [128, 2, 16], BF16))
        for s in range(2):
            K.tr(K.bank[1][:, s * 16:(s + 1) * 16], K.bankb[1], crow[:, s, :], g.ident_f[0:16, 0:16], [rb, g.cb])
        K.op("act", lambda e: e.activation(out=g_scT[:].rearrange("p s k -> p (s k)"), in_=K.bank[1][:, 0:32], func=AF.Silu), reads=[K.bankb[1]], writes=[rb])
        wm = [K.sb(es2, "wm%d" % i, [128, 16, 1024], BF16) for i in range(2)]
        dm = [K.ds() for _ in range(2)]
        wmod = g.w["w_mod"][0]
        for cb in range(12):
            t, b = wm[cb % 2]
            for q4 in range(4):
                K.dma("pool", t[:, q4 * 4:(q4 + 1) * 4, :], kc_view(wmod, cb * 1024, (cb + 1) * 1024)[:, q4 * 4:(q4 + 1) * 4, :], dm[cb % 2], writes=[b])
            for jl in range(8):
                j = cb * 8 + jl
                for kc in range(KC):
                    K.mm(K.bank[2][:, 2 * j:2 * j + 2], K.bankb[2], t[:, kc, jl * 128:(jl + 1) * 128], g_scT[:, :, kc], kc == 0, kc == KC - 1, [b, rb])
        for s in range(2):
            K.op("dve", lambda e: e.tensor_tensor(out=g.modT[:, s, :], in0=K.bank[2][:, 0:192].rearrange("p (j s) -> p s j", s=2)[:, s, :], in1=g.vecT[:, 0:96], op=ALU.add),
                 reads=[K.bankb[2], g.cb], writes=[g.cb])
        for s in range(2):
            K.op("dve", lambda e: e.scalar_tensor_tensor(out=g.AB[:, s, 0, :], in0=g.modT[:, s, 16:32], scalar=1.0, in1=g.vecT[:, 96:112], op0=ALU.add, op1=ALU.mult), reads=[g.cb], writes=[g.cb])
            K.op("dve", lambda e: e.tensor_copy(out=g.AB[:, s, 1, :], in_=g.modT[:, s, 0:16]), reads=[g.cb], writes=[g.cb])
            K.op("dve", lambda e: e.scalar_tensor_tensor(out=g.AB[:, s, 2, :], in0=g.modT[:, s, 64:80], scalar=1.0, in1=g.vecT[:, 112:128], op0=ALU.add, op1=ALU.mult), reads=[g.cb], writes=[g.cb])
            K.op("dve", lambda e: e.tensor_copy(out=g.AB[:, s, 3, :], in_=g.modT[:, s, 48:64]), reads=[g.cb], writes=[g.cb])
        K.barrier()


def make_gate_bcast(K, g, es, s, which, name):
    t, b = K.sb(es, name, [128, D], F32)
    base = 32 if which == 0 else 80
    with ExitStack() as es2:
        rep, rb = K.sb(es2, name + "_rep", [128, 2, 128], F32)
        for c in range(KC):
            bk = 3 + (c // 4) % 2
            K.op("dve", lambda e: e.tensor_copy(out=rep[:, c % 2, :], in_=g.modT[:, s, base + c:base + c + 1].broadcast_to([128, 128])), reads=[g.cb], writes=[rb])
            K.mm(K.bank[bk][:, (c % 4) * 128:(c % 4 + 1) * 128], K.bankb[bk], rep[:, c % 2, :], g.ident_f[:], True, True, [rb, g.cb])
            if c % 4 == 3:
                K.op("act", lambda e: e.copy(out=t[:, (c - 3) * 128:(c + 1) * 128], in_=K.bank[bk][:, :]), reads=[K.bankb[bk]], writes=[b])
        K.barrier()
    return t, b


def phase1(K, g, s, T, x_in):
    nc = K.nc
    NB = T // 512
    win = g.wbf["w_in"]
    groups = [(0, 512, "q0"), (512, 512, "q1"), (1024, 512, "kv"), (C_GQ, 512, "gqT"), (C_GK, 512, "gk"),
              (C_GV, 512, "gv0"), (C_GV + 512, 512, "gv1"), (C_GR, 512, "gr0"), (C_GR + 512, 512, "gr1"), (C_LOW, 32, "low")]
    with ExitStack() as es:
        hTs = [K.sb(es, "hT%d" % i, [128, KC, 512], BF16) for i in range(2)]
        dh = [K.ds() for _ in range(2)]
        wsl = [K.sb(es, "wsl%d" % i, [128, KC, 512], BF16) for i in range(3)]
        dw = [K.ds() for _ in range(3)]
        css = [K.sb(es, "cs", [128, 2, 4, 128], F32) for _ in range(2)]
        dcs = [K.ds() for _ in range(2)]
        pending = []
        sq, sqb = K.sb(es, "sq", [128, 512], F32)
        ssh, sshb = K.sb(es, "ssh", [128, 8], F32)
        qn = [K.sb(es, "qn%d" % i, [128, 512], F32) for i in range(2)]
        t1 = [K.sb(es, "t1%d" % i, [128, 512], F32) for i in range(2)]
        t2 = [K.sb(es, "t2%d" % i, [128, 512], F32) for i in range(2)]
        qr = [K.sb(es, "qr%d" % i, [128, 512], BF16) for i in range(3)]
        qTst, qTstb = K.sb(es, "qTst", [128, 8, 512], BF16)
        kTst, kTstb = K.sb(es, "kTst", [128, 2, 512], BF16)
        dqk = [K.ds() for _ in range(2)]
        stb = [K.sb(es, "stb%d" % i, [128, 512], BF16) for i in range(6)]
        dsb = [K.ds() for _ in range(6)]
        stf = [K.sb(es, "stf%d" % i, [128, 512], F32) for i in range(4)]
        dsf = [K.ds() for _ in range(4)]
        cnt = dict(w=0, mb=0, qk=0, stb=0, stf=0)

        def flush(keep):
            while len(pending) > keep:
                pending.pop(0)()

        def load_act(blk):
            t0_ = blk * 512
            hT_, hTb_ = hTs[blk % 2]
            K.dma("sp", hT_[:], g.sc["hT"].rearrange("c p t -> p c t")[:, :, t0_:t0_ + 512], dh[blk % 2], writes=[hTb_])
            cs_, csb_ = css[blk % 2]
            K.dma("sp", cs_[:, 0, :, :], g.d_cos[t0_:t0_ + 512, :].rearrange("(i p) d -> p i d", p=128), dcs[blk % 2], writes=[csb_])
            K.dma("sp", cs_[:, 1, :, :], g.d_sin[t0_:t0_ + 512, :].rearrange("(i p) d -> p i d", p=128), dcs[blk % 2], writes=[csb_])

        def load_w(gi_global):
            c0, ncol, _ = groups[gi_global % len(groups)]
            j = cnt["w"] % 3
            cnt["w"] += 1
            t, b = wsl[j]
            K.dma("sp", t[:, :, 0:ncol], kc_view(win, c0, c0 + ncol), dw[j], writes=[b])
            return t, b

        def qk_post(bk, nh, hbase, gb, i, stage, stageb, cs, csb):
            W = nh * 128
            P = K.bank[bk][:, 0:W]
            P3 = P.rearrange("p (h d) -> p h d", h=nh)
            j = cnt["qk"] % 2
            cnt["qk"] += 1
            qn_t, qn_b = qn[j]
            t1_t, t1_b = t1[j]
            t2_t, t2_b = t2[j]
            qr_t, qr_b = qr[(cnt["qk"] - 1) % 3]
            K.op("act", lambda e: e.activation(out=sq[:, 0:W], in_=P, func=AF.Square), reads=[K.bankb[bk]], writes=[sqb])
            K.op("dve", lambda e: e.tensor_reduce(out=ssh[:, 0:nh], in_=sq[:, 0:W].rearrange("p (h d) -> p h d", h=nh), axis=AX.X, op=ALU.add), reads=[sqb], writes=[sshb])
            K.op("dve", lambda e: e.tensor_scalar(out=ssh[:, 0:nh], in0=ssh[:, 0:nh], scalar1=1.0 / 128, scalar2=EPS, op0=ALU.mult, op1=ALU.add), reads=[sshb], writes=[sshb])
            K.op("act", lambda e: e.activation(out=ssh[:, 0:nh], in_=ssh[:, 0:nh], func=AF.Sqrt), reads=[sshb], writes=[sshb])
            K.op("dve", lambda e: e.reciprocal(out=ssh[:, 4:4 + nh], in_=ssh[:, 0:nh]), reads=[sshb], writes=[sshb])
            qn3 = qn_t[:, 0:W].rearrange("p (h d) -> p h d", h=nh)
            K.op("dve", lambda e: e.tensor_tensor(out=qn3, in0=P3, in1=ssh[:, 4:4 + nh].unsqueeze(2).broadcast_to([128, nh, 128]), op=ALU.mult),
                 reads=[K.bankb[bk], sshb], writes=[qn_b])
            K.op("pool", lambda e: e.tensor_tensor(out=qn3, in0=qn3, in1=gb[:].unsqueeze(1).broadcast_to([128, nh, 128]), op=ALU.mult), reads=[qn_b, g.cb], writes=[qn_b])
            K.op("pool", lambda e: e.tensor_tensor(out=t1_t[:, 0:W].rearrange("p (h d) -> p h d", h=nh), in0=qn3, in1=cs[:, 0, i, :].unsqueeze(1).broadcast_to([128, nh, 128]), op=ALU.mult),
                 reads=[qn_b, csb], writes=[t1_b])
            qn5 = qn_t[:, 0:W].rearrange("p (h a f i) -> p h a f i", h=nh, a=2, f=2)
            t25 = t2_t[:, 0:W].rearrange("p (h a f i) -> p h a f i", h=nh, a=2, f=2)
            sn4 = cs[:, 1, i, :].rearrange("p (a f i) -> p a f i", a=2, f=2)
            for f in range(2):
                K.op("dve", lambda e: e.tensor_tensor(out=t25[:, :, :, f, :], in0=qn5[:, :, :, 1 - f, :], in1=sn4[:, :, f, :].unsqueeze(1).broadcast_to([128, nh, 2, 32]), op=ALU.mult),
                     reads=[qn_b, csb], writes=[t2_b])
            K.op("dve", lambda e: e.tensor_tensor(out=qr_t[:, 0:W], in0=t1_t[:, 0:W], in1=t2_t[:, 0:W], op=ALU.add), reads=[t1_b, t2_b], writes=[qr_b])

            def tail():
                tb = K.bank[6][:].bitcast(BF16)
                for h in range(nh):
                    K.tr(tb[:, h * 128:(h + 1) * 128], K.bankb[6], qr_t[:, h * 128:(h + 1) * 128], g.ident_b[:], [qr_b, g.cb])
                K.op("act", lambda e: e.copy(out=stage[:, hbase:hbase + nh, i * 128:(i + 1) * 128], in_=tb[:, 0:W].rearrange("p (h t) -> p h t", h=nh)),
                     reads=[K.bankb[6]], writes=[stageb])

            pending.append(tail)

        def store_b(fn_evac, dst, ncols=512):
            j = cnt["stb"] % 6
            cnt["stb"] += 1
            t, b = stb[j]
            fn_evac(t, b)
            K.dma("sp", dst, t[:, 0:ncols], dsb[j], reads=[b])

        def store_f(fn_evac, dst, npart=128):
            j = cnt["stf"] % 4
            cnt["stf"] += 1
            t, b = stf[j]
            fn_evac(t, b)
            K.dma("sp", dst, t[0:npart, :], dsf[j], reads=[b])

        load_act(0)
        wq = [load_w(0), load_w(1)]
        for blk in range(NB):
            t0 = blk * 512
            hT, hTb = hTs[blk % 2]
            cs, csb = css[blk % 2]
            if blk + 1 < NB:
                load_act(blk + 1)
            for gi, (c0, ncol, kind) in enumerate(groups):
                wt, wb = wq.pop(0)
                if gi + 2 < len(groups) or blk + 1 < NB:
                    wq.append(load_w(gi + 2))
                if kind in ("gqT", "gk"):
                    dstT = g.sc["gqT"] if kind == "gqT" else g.sc["gkT"]
                    for m in range(4):
                        bk = 2 + cnt["mb"] % 4
                        cnt["mb"] += 1
                        for kc in range(KC):
                            K.mm(K.bank[bk][:, :], K.bankb[bk], wt[:, kc, m * 128:(m + 1) * 128], hT[:, kc, :], kc == 0, kc == KC - 1, [wb, hTb])
                        sc_ = (128.0 ** -0.5) if kind == "gqT" else 1.0
                        store_b(lambda t, b: K.op("act", lambda e: e.activation(out=t[:], in_=K.bank[bk][:, :], func=AF.Copy, scale=sc_), reads=[K.bankb[bk]], writes=[b]),
                                dstT[m, :, t0:t0 + 512])
                if kind == "low":
                    for dr in range(2):
                        bk = 2 + cnt["mb"] % 4
                        cnt["mb"] += 1
                        for kc in range(KC):
                            K.mm(K.bank[bk][0:16, :], K.bankb[bk], wt[:, kc, dr * 16:(dr + 1) * 16], hT[:, kc, :], kc == 0, kc == KC - 1, [wb, hTb])
                        store_f(lambda t, b: K.op("act", lambda e: e.copy(out=t[0:16, :], in_=K.bank[bk][0:16, :]), reads=[K.bankb[bk]], writes=[b]),
                                g.sc["lowT"][dr, :, t0:t0 + 512], npart=16)
                if kind in ("q0", "q1", "kv", "gk", "gv0", "gv1", "gr0", "gr1"):
                    for i in range(4):
                        bk = 2 + cnt["mb"] % 4
                        cnt["mb"] += 1
                        for kc in range(KC):
                            K.mm(K.bank[bk][:, :], K.bankb[bk], hT[:, kc, i * 128:(i + 1) * 128], wt[:, kc, :], kc == 0, kc == KC - 1, [wb, hTb])
                        flush(1)
                        r0 = t0 + i * 128
                        if kind == "q0":
                            qk_post(bk, 4, 0, g.gq_b, i, qTst, qTstb, cs, csb)
                        elif kind == "q1":
                            qk_post(bk, 4, 4, g.gq_b, i, qTst, qTstb, cs, csb)
                        elif kind == "kv":
                            qk_post(bk, 2, 0, g.gk_b, i, kTst, kTstb, cs, csb)
                            store_b(lambda t, b: K.op("dve", lambda e: e.tensor_copy(out=t[:, 0:256], in_=K.bank[bk][:, 256:512]), reads=[K.bankb[bk]], writes=[b]),
                                    g.sc["v"][r0:r0 + 128, :], ncols=256)
                        elif kind == "gk":
                            store_b(lambda t, b: K.op("dve", lambda e: e.tensor_copy(out=t[:], in_=K.bank[bk][:, :]), reads=[K.bankb[bk]], writes=[b]),
                                    g.sc["gk"][r0:r0 + 128, :])
                        elif kind in ("gv0", "gv1"):
                            o = 0 if kind == "gv0" else 512
                            store_b(lambda t, b: K.op("dve", lambda e: e.tensor_copy(out=t[:], in_=K.bank[bk][:, :]), reads=[K.bankb[bk]], writes=[b]),
                                    g.sc["gv"][r0:r0 + 128, o:o + 512])
                        else:
                            o = 0 if kind == "gr0" else 512
                            store_f(lambda t, b: K.op("act", lambda e: e.activation(out=t[:], in_=K.bank[bk][:, :], func=AF.Silu), reads=[K.bankb[bk]], writes=[b]),
                                    g.sc["sr"][r0:r0 + 128, o:o + 512])
            flush(0)
            K.dma("sp", g.sc["qT"].rearrange("h d t -> d h t")[:, :, t0:t0 + 512], qTst[:], dqk[0], reads=[qTstb])
            K.dma("sp", g.sc["kT"].rearrange("h d t -> d h t")[:, :, t0:t0 + 512], kTst[:], dqk[1], reads=[kTstb])
        K.barrier()


def phase_attn(K, g, T):
    nc = K.nc
    NT = T // 128
    NQB = T // 512
    scale = 128.0 ** -0.5
    with ExitStack() as es:
        kT, kTb = K.sb(es, "a_kT", [128, 2, T], BF16)
        v, vb = K.sb(es, "a_v", [128, NT, 256], BF16)
        d0 = K.ds()
        K.dma("sp", kT[:], g.sc["kT"].rearrange("h d t -> d h t")[:, :, 0:T], d0, writes=[kTb])
        K.dma("sp", v[:], g.sc["v"][0:T, :].rearrange("(n p) d -> p n d", p=128), d0, writes=[vb])
        qs = [K.sb(es, "a_q%d" % i, [128, 8, 512], BF16) for i in range(2)]
        dq = [K.ds() for _ in range(2)]
        pT = [K.sb(es, "a_p%d" % i, [128, 512], BF16) for i in range(4)]
        rd, rdb = K.sb(es, "a_rd", [128, 512], F32)
        ost = [K.sb(es, "a_o%d" % i, [128, 512], BF16) for i in range(2)]
        dso = [K.ds() for _ in range(2)]
        it = 0
        for qb in range(NQB):
            q0 = qb * 512
            qt, qtb = qs[qb % 2]
            K.dma("sp", qt[:], g.sc["qT"].rearrange("h d t -> d h t")[:, :, q0:q0 + 512], dq[qb % 2], writes=[qtb])
            for h in range(8):
                kvh = h // 4
                bo = it % 2
                bd = 2 + it % 2

                def score(sc):
                    bs = 4 + sc % 4
                    K.mm(K.bank[bs][:, :], K.bankb[bs], kT[:, kvh, sc * 128:(sc + 1) * 128], qt[:, h, :], True, True, [kTb, qtb])

                score(0)
                score(1)
                for sc in range(NT):
                    if sc + 2 < NT:
                        score(sc + 2)
                    bs = 4 + sc % 4
                    p, pb = pT[sc % 4]
                    K.op("act", lambda e: e.activation(out=p[:], in_=K.bank[bs][:, :], func=AF.Exp, scale=scale, bias=g.nshift[:, 0:1]),
                         reads=[K.bankb[bs], g.cb], writes=[pb])
                    K.mm(K.bank[bo][:, :], K.bankb[bo], v[:, sc, kvh * 128:(kvh + 1) * 128], p[:], sc == 0, sc == NT - 1, [vb, pb])
                    K.mm(K.bank[bd][:, :], K.bankb[bd], g.ones_b[:], p[:], sc == 0, sc == NT - 1, [g.cb, pb])
                K.op("dve", lambda e: e.reciprocal(out=rd[:], in_=K.bank[bd][:, :]), reads=[K.bankb[bd]], writes=[rdb])
                o, ob = ost[it % 2]
                K.op("dve", lambda e: e.tensor_tensor(out=o[:], in0=K.bank[bo][:, :], in1=rd[:], op=ALU.mult), reads=[K.bankb[bo], rdb], writes=[ob])
                K.dma("sp", g.sc["attT"][h, :, q0:q0 + 512], o[:], dso[it % 2], reads=[ob])
                it += 1
        K.barrier()


def phase_gla(K, g, T, direction):
    nc = K.nc
    NB = T // 512
    U = g.tri[:, 0 + direction, :]
    Cm = g.tri[:, 2 + direction, :]
    M = g.masks[:, direction, :]
    dc = 127 if direction == 0 else 0
    wa = g.wa[direction]
    back = direction == 1
    with ExitStack() as es:
        S, Sb = K.sb(es, "g_S", [128, 4, 256], F32)
        Sh, Shb = K.sb(es, "g_Sh", [128, 4, 256], BF16)
        K.op("dve", lambda e: e.memset(S[:], 0.0), writes=[Sb])
        K.op("pool", lambda e: e.memset(Sh[:], 0.0), writes=[Shb])
        nsl = 2
        lw = [K.sb(es, "g_lw", [17, 512], F32) for i in range(nsl)]
        for t, b in lw:
            K.op("dve", lambda e: e.memset(t[:], 1.0), writes=[b])
        gq = [K.sb(es, "g_gq", [128, 4, 512], BF16) for i in range(nsl)]
        gkT = [K.sb(es, "g_gkT", [128, 4, 512], BF16) for i in range(nsl)]
        gk = [K.sb(es, "g_gk", [128, 4, 512], BF16) for i in range(nsl)]
        gv = [K.sb(es, "g_gv", [128, 4, 1024], BF16) for i in range(nsl)]
        dld = [K.ds() for _ in range(nsl)]
        if back:
            sr = [K.sb(es, "g_sr", [128, 4, 1024], F32) for i in range(nsl)]
            of = [K.sb(es, "g_of", [128, 4, 1024], F32) for i in range(nsl)]
            ggb, ggbb = K.sb(es, "g_ggb", [128, 1024], F32)
            K.dma("sp", ggb[:], g.w["g_gla"][0].partition_broadcast(128), dld[0], writes=[ggbb])
            glst, glstb = K.sb(es, "g_glst", [128, 8, 512], BF16)
            dgl = K.ds()
            sqo, sqob = K.sb(es, "g_sqo", [128, 1024], F32)
            ss4, ss4b = K.sb(es, "g_ss4", [128, 8], F32)
            gl, glb = K.sb(es, "g_gl", [128, 1024], BF16)
        ab_ = [K.sb(es, "g_abs", [128, 512], F32) for _ in range(2)]
        ex = [K.sb(es, "g_ex", [128, 512], F32) for _ in range(2)]
        la = [K.sb(es, "g_la", [128, 512], F32) for _ in range(2)]
        Ec = [K.sb(es, "g_Ec", [128, 512], F32) for _ in range(2)]
        kk = [K.sb(es, "g_kk", [128, 512], BF16) for _ in range(3)]
        Ep = [K.sb(es, "g_Ep", [128, 4, 128], F32) for _ in range(3)]
        Em = [K.sb(es, "g_Em", [128, 4, 128], F32) for _ in range(2)]
        qe = [K.sb(es, "g_qe", [128, 4, 128], BF16) for _ in range(3)]
        ke = [K.sb(es, "g_ke", [128, 4, 128], BF16) for _ in range(2)]
        Am = [K.sb(es, "g_Am", [128, 4, 128], BF16) for _ in range(2)]
        oo = [K.sb(es, "g_oo", [128, 1024], F32) for i in range(2)]
        doo = [K.ds() for _ in range(2)]
        blocks = list(range(NB))
        if back:
            blocks = blocks[::-1]
        order = []
        for bi, blk in enumerate(blocks):
            tl = list(range(4))
            if back:
                tl = tl[::-1]
            for i in tl:
                order.append((bi, blk, i))
        NTL = len(order)

        def loads(bi):
            blk = blocks[bi]
            t0 = blk * 512
            sl = bi % nsl
            K.dma("sp", lw[sl][0][0:16, :], g.sc["lowT"][direction, :, t0:t0 + 512], dld[sl], writes=[lw[sl][1]])
            K.dma("sp", gq[sl][0][:], g.sc["gqT"].rearrange("h d t -> d h t")[:, :, t0:t0 + 512], dld[sl], writes=[gq[sl][1]])
            K.dma("sp", gkT[sl][0][:], g.sc["gkT"].rearrange("h d t -> d h t")[:, :, t0:t0 + 512], dld[sl], writes=[gkT[sl][1]])
            K.dma("sp", gk[sl][0][:], g.sc["gk"][t0:t0 + 512, :].rearrange("(i p) d -> p i d", p=128), dld[sl], writes=[gk[sl][1]])
            K.dma("sp", gv[sl][0][:], g.sc["gv"][t0:t0 + 512, :].rearrange("(i p) d -> p i d", p=128), dld[sl], writes=[gv[sl][1]])
            if back:
                K.dma("sp", sr[sl][0][:], g.sc["sr"][t0:t0 + 512, :].rearrange("(i p) d -> p i d", p=128), dld[sl], writes=[sr[sl][1]])
                K.dma("sp", of[sl][0][:], g.sc["of"][t0:t0 + 512, :].rearrange("(i p) d -> p i d", p=128), dld[sl], writes=[of[sl][1]])

        def stA(n):
            bi, blk, i = order[n]
            sl = bi % nsl
            c0 = i * 128
            lw_t, lw_b = lw[sl]
            ab_t, ab_b = ab_[n % 2]
            ex_t, ex_b = ex[n % 2]
            la_t, la_b = la[n % 2]
            K.mm(K.bank[0][:, :], K.bankb[0], lw_t[0:17, c0:c0 + 128], wa[0:17, :], True, True, [lw_b, g.cb])
            K.op("act", lambda e: e.activation(out=ab_t[:], in_=K.bank[0][:, :], func=AF.Abs), reads=[K.bankb[0]], writes=[ab_b])
            K.op("act", lambda e: e.activation(out=ex_t[:], in_=ab_t[:], func=AF.Exp, scale=-1.0), reads=[ab_b], writes=[ex_b])
            K.op("act", lambda e: e.activation(out=ex_t[:], in_=ex_t[:], func=AF.Ln, bias=1.0), reads=[ex_b], writes=[ex_b])
            K.op("dve", lambda e: e.scalar_tensor_tensor(out=la_t[:], in0=K.bank[0][:, :], scalar=0.0, in1=ex_t[:], op0=ALU.min, op1=ALU.subtract),
                 reads=[K.bankb[0], ex_b], writes=[la_b])

        def stB(n):
            bi, blk, i = order[n]
            sl = bi % nsl
            c0 = i * 128
            la_t, la_b = la[n % 2]
            Ec_t, Ec_b = Ec[n % 2]
            kk_t, kk_b = kk[n % 3]
            Ep_t, Ep_b = Ep[n % 3]
            Em_t, Em_b = Em[n % 2]
            qe_t, qe_b = qe[n % 3]
            ke_t, ke_b = ke[n % 2]
            K.mm(K.bank[1][:, :], K.bankb[1], Cm, la_t[:], True, True, [la_b, g.cb])
            for h in range(4):
                K.mm(K.bank[2][:, h * 128:(h + 1) * 128], K.bankb[2], la_t[:, h * 128:(h + 1) * 128], U, True, True, [la_b, g.cb])
            K.op("act", lambda e: e.activation(out=Ec_t[:], in_=K.bank[1][:, :], func=AF.Exp), reads=[K.bankb[1]], writes=[Ec_b])
            K.op("act", lambda e: e.activation(out=Ep_t[:].rearrange("p h t -> p (h t)"), in_=K.bank[2][:, :], func=AF.Exp), reads=[K.bankb[2]], writes=[Ep_b])
            K.op("act", lambda e: e.activation(out=Em_t[:].rearrange("p h t -> p (h t)"), in_=K.bank[2][:, :], func=AF.Exp, scale=-1.0), reads=[K.bankb[2]], writes=[Em_b])
            K.op("dve", lambda e: e.tensor_tensor(out=kk_t[:], in0=gk[sl][0][:, i, :], in1=Ec_t[:], op=ALU.mult), reads=[gk[sl][1], Ec_b], writes=[kk_b])
            K.op("dve", lambda e: e.tensor_tensor(out=qe_t[:], in0=gq[sl][0][:, :, c0:c0 + 128], in1=Ep_t[:], op=ALU.mult), reads=[gq[sl][1], Ep_b], writes=[qe_b])
            K.op("dve", lambda e: e.tensor_tensor(out=ke_t[:], in0=gkT[sl][0][:, :, c0:c0 + 128], in1=Em_t[:], op=ALU.mult), reads=[gkT[sl][1], Em_b], writes=[ke_b])

        def stC(n):
            qe_t, qe_b = qe[n % 3]
            ke_t, ke_b = ke[n % 2]
            Am_t, Am_b = Am[n % 2]
            for h in range(4):
                K.mm(K.bank[3][:, h * 128:(h + 1) * 128], K.bankb[3], ke_t[:, h, :], qe_t[:, h, :], True, True, [ke_b, qe_b])
            K.op("dve", lambda e: e.tensor_tensor(out=Am_t[:], in0=K.bank[3][:, :].rearrange("p (h t) -> p h t", h=4), in1=M.unsqueeze(1).broadcast_to([128, 4, 128]), op=ALU.mult),
                 reads=[K.bankb[3], g.cb], writes=[Am_b])

        def stD(n):
            bi, blk, i = order[n]
            sl = bi % nsl
            c0 = i * 128
            t0 = blk * 512
            kk_t, kk_b = kk[n % 3]
            Ep_t, Ep_b = Ep[n % 3]
            qe_t, qe_b = qe[n % 3]
            Am_t, Am_b = Am[n % 2]
            gv_t, gv_b = gv[sl]
            for h in range(4):
                bs = 6 + h // 2
                K.mm(K.bank[bs][:, (h % 2) * 256:(h % 2 + 1) * 256], K.bankb[bs], kk_t[:, h * 128:(h + 1) * 128], gv_t[:, i, h * 256:(h + 1) * 256], True, True, [kk_b, gv_b])
            for h in range(4):
                bo = 4 + h // 2
                osl = K.bank[bo][:, (h % 2) * 256:(h % 2 + 1) * 256]
                K.mm(osl, K.bankb[bo], Am_t[:, h, :], gv_t[:, i, h * 256:(h + 1) * 256], True, False, [Am_b, gv_b])
                K.mm(osl, K.bankb[bo], qe_t[:, h, :], Sh[:, h, :], False, True, [qe_b, Shb])
            for h in range(4):
                bs = 6 + h // 2
                K.op("dve", lambda e: e.scalar_tensor_tensor(out=S[:, h, :], in0=S[:, h, :], scalar=Ep_t[:, h, dc:dc + 1], in1=K.bank[bs][:, (h % 2) * 256:(h % 2 + 1) * 256],
                                                             op0=ALU.mult, op1=ALU.add), reads=[Sb, Ep_b, K.bankb[bs]], writes=[Sb])
            K.op("pool", lambda e: e.tensor_copy(out=Sh[:], in_=S[:]), reads=[Sb], writes=[Shb])
            o_t, o_b = oo[n % 2]
            r0 = t0 + c0
            if not back:
                for hh in range(2):
                    K.op("act", lambda e: e.copy(out=o_t[:, hh * 512:(hh + 1) * 512], in_=K.bank[4 + hh][:, :]), reads=[K.bankb[4 + hh]], writes=[o_b])
                K.dma("sp", g.sc["of"][r0:r0 + 128, :], o_t[:], doo[n % 2], reads=[o_b])
            else:
                sr_t, sr_b = sr[sl]
                of_t, of_b = of[sl]
                for hh in range(2):
                    K.op("dve", lambda e: e.tensor_tensor(out=o_t[:, hh * 512:(hh + 1) * 512], in0=K.bank[4 + hh][:, :], in1=of_t[:, i, hh * 512:(hh + 1) * 512], op=ALU.add),
                         reads=[K.bankb[4 + hh], of_b], writes=[o_b])
                K.op("act", lambda e: e.activation(out=sqo[:], in_=o_t[:], func=AF.Square), reads=[o_b], writes=[sqob])
                K.op("dve", lambda e: e.tensor_reduce(out=ss4[:, 0:4], in_=sqo[:].rearrange("p (h d) -> p h d", h=4), axis=AX.X, op=ALU.add), reads=[sqob], writes=[ss4b])
                K.op("dve", lambda e: e.tensor_scalar(out=ss4[:, 0:4], in0=ss4[:, 0:4], scalar1=1.0 / 256, scalar2=EPS, op0=ALU.mult, op1=ALU.add), reads=[ss4b], writes=[ss4b])
                K.op("act", lambda e: e.activation(out=ss4[:, 0:4], in_=ss4[:, 0:4], func=AF.Sqrt), reads=[ss4b], writes=[ss4b])
                K.op("dve", lambda e: e.reciprocal(out=ss4[:, 4:8], in_=ss4[:, 0:4]), reads=[ss4b], writes=[ss4b])
                o3 = o_t[:].rearrange("p (h d) -> p h d", h=4)
                K.op("pool", lambda e: e.tensor_tensor(out=o3, in0=o3, in1=ss4[:, 4:8].unsqueeze(2).broadcast_to([128, 4, 256]), op=ALU.mult), reads=[o_b, ss4b], writes=[o_b])
                K.op("pool", lambda e: e.tensor_tensor(out=o_t[:], in0=o_t[:], in1=ggb[:], op=ALU.mult), reads=[o_b, ggbb], writes=[o_b])
                K.op("pool", lambda e: e.tensor_tensor(out=gl[:], in0=o_t[:], in1=sr_t[:, i, :], op=ALU.mult), reads=[o_b, sr_b], writes=[glb])
                tb = K.bank[0][:].bitcast(BF16)
                for j in range(8):
                    K.tr(tb[:, j * 128:(j + 1) * 128], K.bankb[0], gl[:, j * 128:(j + 1) * 128], g.ident_b[:], [glb, g.cb])
                K.op("act", lambda e: e.copy(out=glst[:, :, c0:c0 + 128], in_=tb[:, 0:1024].rearrange("p (j t) -> p j t", j=8)), reads=[K.bankb[0]], writes=[glstb])
                last_i = 0 if back else 3
                if i == last_i:
                    K.dma("sp", g.sc["glaT"].rearrange("j f t -> f j t")[:, :, t0:t0 + 512], glst[:], dgl, reads=[glstb])
            last_i = 0 if back else 3
            if i == last_i and bi + 2 < NB:
                loads(bi + 2)

        loads(0)
        if NB > 1:
            loads(1)
        for n in range(min(3, NTL)):
            pass
        if NTL > 0:
            stA(0)
        if NTL > 1:
            stA(1)
        if NTL > 0:
            stB(0)
        if NTL > 2:
            stA(2)
        if NTL > 1:
            stB(1)
        if NTL > 0:
            stC(0)
        for n in range(NTL):
            if n + 3 < NTL:
                stA(n + 3)
            if n + 2 < NTL:
                stB(n + 2)
            if n + 1 < NTL:
                stC(n + 1)
            stD(n)
        K.barrier()


def phase3(K, g, s, T, x_in):
    nc = K.nc
    NB = T // 512
    win, wbra, wbrg, wout = g.wbf["w_in"], g.wbf["w_br_att"], g.wbf["w_br_gla"], g.wbf["w_out"]
    with ExitStack() as es:
        gt1, gt1b = make_gate_bcast(K, g, es, s, 0, "gt1")
        hT, hTb = K.sb(es, "m_hT", [128, KC, 512], BF16)
        at, atb = K.sb(es, "m_at", [128, 8, 512], BF16)
        gl, glb = K.sb(es, "m_gl", [128, 8, 512], BF16)
        dact = [K.ds() for _ in range(3)]
        wga = [K.sb(es, "m_wga%d" % i, [128, KC, 256], BF16) for i in range(2)]
        wgg = [K.sb(es, "m_wgg%d" % i, [128, KC, 256], BF16) for i in range(2)]
        wba = [K.sb(es, "m_wba%d" % i, [128, 8, 256], BF16) for i in range(2)]
        wbg = [K.sb(es, "m_wbg%d" % i, [128, 8, 256], BF16) for i in range(2)]
        dwt = [K.ds() for _ in range(2)]
        wo = [K.sb(es, "m_wo%d" % i, [128, KC, 512], BF16) for i in range(2)]
        dwo = [K.ds() for _ in range(2)]
        mg, mgb = K.sb(es, "m_mg", [128, KC, 512], BF16)
        sga = [K.sb(es, "m_sga%d" % i, [128, 512], F32) for i in range(2)]
        sgg = [K.sb(es, "m_sgg%d" % i, [128, 512], F32) for i in range(2)]
        tmp, tmpb = K.sb(es, "m_tmp", [128, 512], F32)
        tmp2, tmp2b = K.sb(es, "m_tmp2", [128, 512], F32)
        xblk = es.enter_context(nc.sbuf_tensor(K.nm("m_xblk"), [128, 4, D], F32))
        xb = [Buf() for _ in range(4)]
        dx = [K.ds() for _ in range(4)]
        dxs = [K.ds() for _ in range(4)]
        cnt = dict(wt=0, wo=0, ch=0, mb=0)

        def load_gate_w(gi):
            j = cnt["wt"] % 2
            cnt["wt"] += 1
            K.dma("sp", wga[j][0][:], kc_view(win, C_GA + gi * 256, C_GA + (gi + 1) * 256), dwt[j], writes=[wga[j][1]])
            K.dma("sp", wgg[j][0][:], kc_view(win, C_GG + gi * 256, C_GG + (gi + 1) * 256), dwt[j], writes=[wgg[j][1]])
            K.dma("sp", wba[j][0][:], kc_view(wbra, gi * 256, (gi + 1) * 256), dwt[j], writes=[wba[j][1]])
            K.dma("sp", wbg[j][0][:], kc_view(wbrg, gi * 256, (gi + 1) * 256), dwt[j], writes=[wbg[j][1]])
            return j

        def load_wo(cg):
            j = cnt["wo"] % 2
            cnt["wo"] += 1
            K.dma("sp", wo[j][0][:], kc_view(wout, cg * 512, (cg + 1) * 512), dwo[j], writes=[wo[j][1]])
            return j

        def load_acts(blk_):
            t0_ = blk_ * 512
            K.dma("sp", hT[:], g.sc["hT"].rearrange("c p t -> p c t")[:, :, t0_:t0_ + 512], dact[0], writes=[hTb])
            K.dma("sp", at[:], g.sc["attT"].rearrange("c p t -> p c t")[:, :, t0_:t0_ + 512], dact[1], writes=[atb])
            K.dma("sp", gl[:], g.sc["glaT"].rearrange("c p t -> p c t")[:, :, t0_:t0_ + 512], dact[2], writes=[glb])

        load_acts(0)
        wq = [load_gate_w(0)]
        for blk in range(NB):
            t0 = blk * 512
            for gi in range(8):
                j = wq.pop(0)
                if gi + 1 < 8:
                    wq.append(load_gate_w(gi + 1))
                if gi == 4:
                    for i in range(4):
                        K.dma("sp", xblk[:, i, :], x_in[t0 + i * 128:t0 + (i + 1) * 128, :], dx[i], writes=[xb[i]])
                for cc in range(2):
                    ch = gi * 2 + cc
                    b0 = (cnt["ch"] % 2) * 4
                    sa, sab = sga[cnt["ch"] % 2]
                    sg_, sgb = sgg[cnt["ch"] % 2]
                    cnt["ch"] += 1
                    cs_ = slice(cc * 128, (cc + 1) * 128)
                    for kc in range(KC):
                        K.mm(K.bank[b0][:, :], K.bankb[b0], wga[j][0][:, kc, cs_], hT[:, kc, :], kc == 0, kc == KC - 1, [wga[j][1], hTb])
                    for kc in range(KC):
                        K.mm(K.bank[b0 + 1][:, :], K.bankb[b0 + 1], wgg[j][0][:, kc, cs_], hT[:, kc, :], kc == 0, kc == KC - 1, [wgg[j][1], hTb])
                    for kc in range(8):
                        K.mm(K.bank[b0 + 2][:, :], K.bankb[b0 + 2], wba[j][0][:, kc, cs_], at[:, kc, :], kc == 0, kc == 7, [wba[j][1], atb])
                    for kc in range(8):
                        K.mm(K.bank[b0 + 3][:, :], K.bankb[b0 + 3], wbg[j][0][:, kc, cs_], gl[:, kc, :], kc == 0, kc == 7, [wbg[j][1], glb])
                    K.op("act", lambda e: e.activation(out=sa[:], in_=K.bank[b0][:, :], func=AF.Sigmoid), reads=[K.bankb[b0]], writes=[sab])
                    K.op("act", lambda e: e.activation(out=sg_[:], in_=K.bank[b0 + 1][:, :], func=AF.Sigmoid), reads=[K.bankb[b0 + 1]], writes=[sgb])
                    K.op("dve", lambda e: e.tensor_tensor(out=tmp[:], in0=K.bank[b0 + 2][:, :], in1=sa[:], op=ALU.mult), reads=[K.bankb[b0 + 2], sab], writes=[tmpb])
                    K.op("dve", lambda e: e.tensor_tensor(out=tmp2[:], in0=K.bank[b0 + 3][:, :], in1=sg_[:], op=ALU.mult), reads=[K.bankb[b0 + 3], sgb], writes=[tmp2b])
                    K.op("pool", lambda e: e.tensor_tensor(out=mg[:, ch, :], in0=tmp[:], in1=tmp2[:], op=ALU.add), reads=[tmpb, tmp2b], writes=[mgb])
            wqo = [load_wo(0)]
            if blk + 1 < NB:
                load_acts(blk + 1)
                wq = [load_gate_w(0)]
            for cg in range(4):
                j = wqo.pop(0)
                if cg + 1 < 4:
                    wqo.append(load_wo(cg + 1))
                cols = slice(cg * 512, (cg + 1) * 512)
                for i in range(4):
                    bk = cnt["mb"] % 4
                    cnt["mb"] += 1
                    for kc in range(KC):
                        K.mm(K.bank[bk][:, :], K.bankb[bk], mg[:, kc, i * 128:(i + 1) * 128], wo[j][0][:, kc, :], kc == 0, kc == KC - 1, [mgb, wo[j][1]])
                    K.op("dve", lambda e: e.tensor_tensor(out=tmp[:], in0=K.bank[bk][:, :], in1=gt1[:, cols], op=ALU.mult), reads=[K.bankb[bk], gt1b], writes=[tmpb])
                    K.op("pool", lambda e: e.tensor_tensor(out=xblk[:, i, cols], in0=xblk[:, i, cols], in1=tmp[:], op=ALU.add), reads=[xb[i], tmpb], writes=[xb[i]])
            for i in range(4):
                K.dma("sp", g.sc["x1"][t0 + i * 128:t0 + (i + 1) * 128, :], xblk[:, i, :], dxs[i], reads=[xb[i]])
        K.barrier()


def phase4a(K, g, T):
    nc = K.nc
    wup = g.wbf["w_up"]
    nW = -(-T // 510)
    bounds = [int(round(k * T / nW)) for k in range(nW + 1)]
    W0, W1, W2, WB = 128, 216, 304, 392
    with ExitStack() as es:
        hw = [K.sb(es, "f_h%d" % i, [128, KC, 512], BF16) for i in range(2)]
        dh = [K.ds() for _ in range(2)]
        wv = [K.sb(es, "f_wv%d" % i, [128, KC, 512], BF16) for i in range(2)]
        wg = [K.sb(es, "f_wg%d" % i, [128, KC, 512], BF16) for i in range(2)]
        dw = [K.ds() for _ in range(2)]
        cv = [K.sb(es, "f_cv%d" % i, [128, 512], F32) for i in range(2)]
        cg_ = [K.sb(es, "f_cg%d" % i, [128, 512], F32) for i in range(2)]
        ast = [K.sb(es, "f_ast%d" % i, [128, 4, 512], BF16) for i in range(2)]
        dast = [K.ds() for _ in range(2)]
        cnt = dict(w=0, ch=0, st=0)

        def load_w(gi):
            j = cnt["w"] % 2
            cnt["w"] += 1
            K.dma("sp", wv[j][0][:], kc_view(wup, gi * 512, (gi + 1) * 512), dw[j], writes=[wv[j][1]])
            K.dma("sp", wg[j][0][:], kc_view(wup, DFF + gi * 512, DFF + (gi + 1) * 512), dw[j], writes=[wg[j][1]])
            return j

        def conv(bk, ch, out, outb, Wu):
            vt = g.vecT
            K.op("act", lambda e: e.activation(out=out[:, 0:Wu], in_=K.bank[bk][:, 0:Wu], func=AF.Identity, scale=vt[:, W1 + ch:W1 + ch + 1], bias=vt[:, WB + ch:WB + ch + 1]),
                 reads=[K.bankb[bk], g.cb], writes=[outb])
            K.op("dve", lambda e: e.scalar_tensor_tensor(out=out[:, 1:Wu], in0=K.bank[bk][:, 0:Wu - 1], scalar=vt[:, W0 + ch:W0 + ch + 1], in1=out[:, 1:Wu], op0=ALU.mult, op1=ALU.add),
                 reads=[K.bankb[bk], g.cb, outb], writes=[outb])
            K.op("dve", lambda e: e.scalar_tensor_tensor(out=out[:, 0:Wu - 1], in0=K.bank[bk][:, 1:Wu], scalar=vt[:, W2 + ch:W2 + ch + 1], in1=out[:, 0:Wu - 1], op0=ALU.mult, op1=ALU.add),
                 reads=[K.bankb[bk], g.cb, outb], writes=[outb])

        wq = [load_w(0)]
        for wi in range(nW):
            olo, ohi = bounds[wi], bounds[wi + 1]
            a, b = max(olo - 1, 0), min(ohi + 1, T)
            Wu, lo, hi = b - a, olo - a, ohi - a
            h, hb = hw[wi % 2]
            K.dma("sp", h[:, :, 0:Wu], g.sc["h2T"].rearrange("c p t -> p c t")[:, :, a:b], dh[wi % 2], writes=[hb])
            for gi in range(11):
                j = wq.pop(0)
                if gi + 1 < 11:
                    wq.append(load_w(gi + 1))
                elif wi + 1 < nW:
                    wq.append(load_w(0))
                st, stb_ = ast[cnt["st"] % 2]
                dst_ = dast[cnt["st"] % 2]
                cnt["st"] += 1
                for jj in range(4):
                    ch = gi * 4 + jj
                    k2 = cnt["ch"] % 2
                    cnt["ch"] += 1
                    bv, bg = k2 * 2, k2 * 2 + 1
                    cs_ = slice(jj * 128, (jj + 1) * 128)
                    for kc in range(KC):
                        K.mm(K.bank[bv][:, 0:Wu], K.bankb[bv], wv[j][0][:, kc, cs_], h[:, kc, 0:Wu], kc == 0, kc == KC - 1, [wv[j][1], hb])
                    for kc in range(KC):
                        K.mm(K.bank[bg][:, 0:Wu], K.bankb[bg], wg[j][0][:, kc, cs_], h[:, kc, 0:Wu], kc == 0, kc == KC - 1, [wg[j][1], hb])
                    cvt, cvb = cv[k2]
                    cgt, cgb = cg_[k2]
                    conv(bv, ch, cvt, cvb, Wu)
                    conv(bg, NFC + ch, cgt, cgb, Wu)
                    K.op("act", lambda e: e.activation(out=cgt[:, 0:Wu], in_=cgt[:, 0:Wu], func=AF.Silu), reads=[cgb], writes=[cgb])
                    K.op("pool", lambda e: e.tensor_tensor(out=st[:, jj, 0:Wu], in0=cgt[:, 0:Wu], in1=cvt[:, 0:Wu], op=ALU.mult), reads=[cgb, cvb], writes=[stb_])
                K.dma("sp", g.sc["actT"].rearrange("c p t -> p c t")[:, gi * 4:(gi + 1) * 4, olo:ohi], st[:, :, lo:hi], dst_, reads=[stb_])
        K.barrier()


def phase4b(K, g, s, T, y_out):
    nc = K.nc
    NB = T // 512
    wdn = g.wbf["w_down"]
    with ExitStack() as es:
        gt2, gt2b = make_gate_bcast(K, g, es, s, 1, "gt2")
        gfin, gfinb = K.sb(es, "d_gfin", [128, D], F32)
        d0 = K.ds()
        K.dma("sp", gfin[:], g.w["g_final"].partition_broadcast(128), d0, writes=[gfinb])
        act, _ = K.sb(es, "d_act", [128, NFC, 512], BF16)
        actbs = [Buf(), Buf()]
        dact = [K.ds(), K.ds()]
        wd = [K.sb(es, "d_wd%d" % i, [128, NFC, 512], BF16) for i in range(2)]
        dwd = [K.ds() for _ in range(2)]
        xblk = es.enter_context(nc.sbuf_tensor(K.nm("d_xblk"), [128, 4, D], F32))
        xb = [Buf() for _ in range(4)]
        dx = [K.ds() for _ in range(4)]
        tmp = [K.sb(es, "d_tmp%d" % i, [128, 512], F32) for i in range(2)]
        ssq, ssqb = K.sb(es, "d_ssq", [128, 8], F32)
        ot = [K.sb(es, "d_ot%d" % i, [128, D], F32) for i in range(1)]
        dot = [K.ds() for _ in range(1)]
        junk, junkb = ot[0]
        cnt = dict(w=0, mb=0, ot=0)

        def load_wd(cg):
            j = cnt["w"] % 2
            cnt["w"] += 1
            K.dma("sp", wd[j][0][:], kc_view(wdn, cg * 512, (cg + 1) * 512), dwd[j], writes=[wd[j][1]])
            return j

        wq = [load_wd(0)]
        for blk in range(NB):
            t0 = blk * 512
            for hf in range(2):
                K.dma("sp", act[:, :, hf * 256:(hf + 1) * 256], g.sc["actT"].rearrange("c p t -> p c t")[:, :, t0 + hf * 256:t0 + (hf + 1) * 256], dact[hf], writes=[actbs[hf]])
            for i in range(4):
                K.dma("sp", xblk[:, i, :], g.sc["x1"][t0 + i * 128:t0 + (i + 1) * 128, :], dx[i], writes=[xb[i]])
            for cg in range(4):
                j = wq.pop(0)
                if cg + 1 < 4:
                    wq.append(load_wd(cg + 1))
                elif blk + 1 < NB:
                    wq.append(load_wd(0))
                cols = slice(cg * 512, (cg + 1) * 512)
                for i in range(4):
                    bk = cnt["mb"] % 4
                    tt, ttb = tmp[cnt["mb"] % 2]
                    cnt["mb"] += 1
                    for kc in range(NFC):
                        K.mm(K.bank[bk][:, :], K.bankb[bk], act[:, kc, i * 128:(i + 1) * 128], wd[j][0][:, kc, :], kc == 0, kc == NFC - 1, [actbs[i // 2], wd[j][1]])
                    K.op("dve", lambda e: e.tensor_tensor(out=tt[:], in0=K.bank[bk][:, :], in1=gt2[:, cols], op=ALU.mult), reads=[K.bankb[bk], gt2b], writes=[ttb])
                    K.op("pool", lambda e: e.tensor_tensor(out=xblk[:, i, cols], in0=xblk[:, i, cols], in1=tt[:], op=ALU.add), reads=[xb[i], ttb], writes=[xb[i]])
            for i in range(4):
                K.op("act", lambda e: e.activation(out=junk[:], in_=xblk[:, i, :], func=AF.Square, accum_out=ssq[:, i:i + 1]), reads=[xb[i]], writes=[junkb, ssqb])
            K.op("dve", lambda e: e.tensor_scalar(out=ssq[:, 0:4], in0=ssq[:, 0:4], scalar1=1.0 / D, scalar2=EPS, op0=ALU.mult, op1=ALU.add), reads=[ssqb], writes=[ssqb])
            K.op("act", lambda e: e.activation(out=ssq[:, 0:4], in_=ssq[:, 0:4], func=AF.Sqrt), reads=[ssqb], writes=[ssqb])
            K.op("dve", lambda e: e.reciprocal(out=ssq[:, 4:8], in_=ssq[:, 0:4]), reads=[ssqb], writes=[ssqb])
            for i in range(4):
                o, ob = ot[0]
                dd = dot[0]
                cnt["ot"] += 1
                K.op("dve", lambda e: e.scalar_tensor_tensor(out=o[:], in0=xblk[:, i, :], scalar=ssq[:, 4 + i:5 + i], in1=gfin[:], op0=ALU.mult, op1=ALU.mult),
                     reads=[xb[i], ssqb, gfinb], writes=[ob])
                K.dma("act", y_out[t0 + i * 128:t0 + (i + 1) * 128, :], o[:], dd, reads=[ob])
        K.barrier()


W_SHAPES = {
    "w_mod": (1, D, 6 * D), "b_mod": (1, 6 * D), "g_mix_norm": (1, D), "w_in": (1, D, IN_COLS), "g_q": (1, 128), "g_k": (1, 128),
    "w_a_up_f": (1, 16, 512), "b_a_f": (1, 512), "w_a_up_b": (1, 16, 512), "b_a_b": (1, 512), "g_gla": (1, 1024),
    "w_br_att": (1, 1024, D), "w_br_gla": (1, 1024, D), "w_out": (1, D, D), "g_ffn_norm": (1, D), "w_up": (1, D, 2 * DFF),
    "w_conv": (1, 3, 2 * DFF), "b_conv": (1, 2 * DFF), "w_down": (1, DFF, D), "g_final": (D,),
}
CONV_W = [("w_in", D, IN_COLS), ("w_br_att", 1024, D), ("w_br_gla", 1024, D), ("w_out", D, D), ("w_up", D, 2 * DFF), ("w_down", DFF, D)]


def build(seq_T, debug=False, phases=None):
    nc = bass.Bass("TRN2", target_bir_lowering=False)
    g = G()
    Tmax = max(seq_T)

    def dt(name, shape, dtype, kind):
        return nc.dram_tensor(name, list(shape), dtype, kind=kind).ap()

    g.w = {n: dt(n, sh, F32, "ExternalInput") for n, sh in W_SHAPES.items()}
    xs = [dt("xin%d" % i, [T, D], F32, "ExternalInput") for i, T in enumerate(seq_T)]
    g.d_c2 = dt("c2", [2, D], F32, "ExternalInput")
    g.d_ident = dt("k_ident", [128, 128], F32, "ExternalInput")
    g.d_masks = dt("k_masks", [2, 128, 128], F32, "ExternalInput")
    g.d_tri = dt("k_tri", [4, 128, 128], F32, "ExternalInput")
    g.d_cos = dt("k_cos", [Tmax, 128], F32, "ExternalInput")
    g.d_sin = dt("k_sin", [Tmax, 128], F32, "ExternalInput")
    ys = [dt("yout%d" % i, [T, D], F32, "ExternalOutput") for i, T in enumerate(seq_T)]
    sk = "ExternalOutput" if debug else "Internal"
    g.wbf = {n: dt("bf_" + n, [Kd, N], BF16, "Internal") for n, Kd, N in CONV_W}
    g.conv_pairs = [(g.w[n][0], g.wbf[n], Kd, N) for n, Kd, N in CONV_W]
    g.sc = {
        "qT": dt("s_qT", [8, 128, Tmax], BF16, sk), "kT": dt("s_kT", [2, 128, Tmax], BF16, sk), "v": dt("s_v", [Tmax, 256], BF16, sk),
        "gqT": dt("s_gqT", [4, 128, Tmax], BF16, sk), "gkT": dt("s_gkT", [4, 128, Tmax], BF16, sk), "gk": dt("s_gk", [Tmax, 512], BF16, sk),
        "gv": dt("s_gv", [Tmax, 1024], BF16, sk), "sr": dt("s_sr", [Tmax, 1024], F32, sk), "lowT": dt("s_lowT", [2, 16, Tmax], F32, sk),
        "hT": dt("s_hT", [KC, 128, Tmax], BF16, sk), "attT": dt("s_attT", [8, 128, Tmax], BF16, sk), "of": dt("s_of", [Tmax, 1024], F32, sk),
        "glaT": dt("s_glaT", [8, 128, Tmax], BF16, sk), "x1": dt("s_x1", [Tmax, D], F32, sk), "h2T": dt("s_h2T", [KC, 128, Tmax], BF16, sk),
        "actT": dt("s_actT", [NFC, 128, Tmax], BF16, sk),
    }
    with ExitStack() as es:
        K = Ctx(nc, es)
        phase_convert(K, g)
        phase_setup(K, g, es)
        for s, T in enumerate(seq_T):
            plist = [("p0", lambda: phase_prep(K, g, s, T, xs[s], g.sc["hT"], 0)), ("p1", lambda: phase1(K, g, s, T, xs[s])), ("attn", lambda: phase_attn(K, g, T)), ("glaf", lambda: phase_gla(K, g, T, 0)),
                     ("glab", lambda: phase_gla(K, g, T, 1)), ("p3", lambda: phase3(K, g, s, T, xs[s])),
                     ("p3b", lambda: phase_prep(K, g, s, T, g.sc["x1"], g.sc["h2T"], 2)), ("p4a", lambda: phase4a(K, g, T)),
                     ("p4b", lambda: phase4b(K, g, s, T, ys[s]))]
            for name, fn in plist:
                if phases is None or name in phases:
                    fn()
        K.barrier()
    return nc


def make_consts(Tmax):
    idx = np.arange(128)
    s_le_t = (idx[:, None] <= idx[None, :]).astype(np.float32)
    s_ge_t = (idx[:, None] >= idx[None, :]).astype(np.float32)
    s_gt_t = (idx[:, None] > idx[None, :]).astype(np.float32)
    s_lt_t = (idx[:, None] < idx[None, :]).astype(np.float32)
    tri = np.stack([s_le_t, s_ge_t, s_gt_t, s_lt_t]) / 16.0
    masks = np.stack([s_le_t, s_gt_t])
    t = np.arange(Tmax)
    row = (t // 64).astype(np.float32)
    col = (t % 64).astype(np.float32)
    inv = (np.float32(10000.0) ** (-np.arange(0, 64, 2, dtype=np.float32) / np.float32(64))).astype(np.float32)
    ar = row[:, None] * inv
    ac = col[:, None] * inv
    ang = np.concatenate([ar, ar, ac, ac], axis=-1).astype(np.float32)
    cos = np.cos(ang).astype(np.float32)
    sin = np.sin(ang).astype(np.float32)
    sgn = np.concatenate([-np.ones(32), np.ones(32), -np.ones(32), np.ones(32)]).astype(np.float32)
    return {"k_ident": np.eye(128, dtype=np.float32), "k_masks": masks.astype(np.float32), "k_tri": tri.astype(np.float32),
            "k_cos": cos, "k_sin": (sin * sgn[None, :]).astype(np.float32)}


_NC_CACHE = {}


def kernel(**inputs):
    seq_T = [inputs["x_prompt"].shape[1], inputs["x_sample"].shape[1]]
    key = tuple(seq_T)
    if key not in _NC_CACHE:
        _NC_CACHE[key] = build(seq_T)
    nc = _NC_CACHE[key]
    consts = make_consts(max(seq_T))
    wts = {n: np.ascontiguousarray(np.asarray(inputs[n], dtype=np.float32)) for n in W_SHAPES}
    in_maps = []
    for i in range(N_CORES):
        m = dict(wts)
        m.update(consts)
        m["xin0"] = np.ascontiguousarray(np.asarray(inputs["x_prompt"][i], dtype=np.float32))
        m["xin1"] = np.ascontiguousarray(np.asarray(inputs["x_sample"][i], dtype=np.float32))
        m["c2"] = np.ascontiguousarray(np.stack([np.asarray(inputs["c_prompt"][i]), np.asarray(inputs["c_sample"][i])]).astype(np.float32))
        in_maps.append(m)
    res = run_bass_kernel_spmd(nc, in_maps, core_ids=list(range(N_CORES)))
    y_p = np.stack([np.asarray(r["yout0"], dtype=np.float32) for r in res.results])
    y_s = np.stack([np.asarray(r["yout1"], dtype=np.float32) for r in res.results])
    return (y_p, y_s)
```
